# Optimizing a Trainium2 kernel written in Bass

```python
import jax, jax.numpy as jnp
from jax import lax
import numpy as np

D_MODEL = 2048
BATCH = 2
SEQ = 4096
DEPTH = 1

CHUNK = 64
MIX_WIDTH = D_MODEL
CONV_WIDTH = MIX_WIDTH // 2
CONV_GROUPS = 16
CONV_K = 3
GLA_WIDTH = MIX_WIDTH - CONV_WIDTH
GLA_HEADS = 4
GLA_DK = GLA_WIDTH // 2
GLA_HEAD_K = GLA_DK // GLA_HEADS
GLA_HEAD_V = GLA_WIDTH // GLA_HEADS
GLA_RANK = 16
GLA_TAU = 16.0
EPS = 1e-6

SPLIT_SIZES = (CONV_WIDTH, CONV_WIDTH, CONV_WIDTH, CONV_WIDTH,
               GLA_DK, GLA_DK, GLA_WIDTH, GLA_WIDTH, GLA_RANK)
IN_COLS = int(sum(SPLIT_SIZES))
SPLIT_IDX = [int(i) for i in np.cumsum(SPLIT_SIZES)[:-1]]

kernel_name = "hybrid_conv_gla_parallel_heads"


def rms_norm(x, g):
    xf = x.astype(jnp.float32)
    y = xf * lax.rsqrt(jnp.mean(xf * xf, axis=-1, keepdims=True) + EPS)
    return (y * g.astype(jnp.float32)).astype(x.dtype)


def short_conv_branch(h, bg, cg, z, conv_w, conv_b):
    u = cg * h
    s = u.shape[1]
    up = jnp.pad(u, ((0, 0), (CONV_K - 1, 0), (0, 0)))
    y = conv_b + sum(conv_w[j] * up[:, j:j + s] for j in range(CONV_K))
    return bg * y * jax.nn.silu(z)


def gla_branch(q, k, v, r, g_down, w_up, b_gate, norm_g):
    bsz, s, _ = q.shape
    n = s // CHUNK
    f32 = jnp.float32
    qc = q.astype(f32).reshape(bsz, n, CHUNK, GLA_HEADS, GLA_HEAD_K) * (GLA_HEAD_K ** -0.5)
    kc = k.astype(f32).reshape(bsz, n, CHUNK, GLA_HEADS, GLA_HEAD_K)
    vc = v.astype(f32).reshape(bsz, n, CHUNK, GLA_HEADS, GLA_HEAD_V)
    glog = jax.nn.log_sigmoid(g_down.astype(f32) @ w_up.astype(f32) + b_gate.astype(f32)) / GLA_TAU
    glog = glog.reshape(bsz, n, CHUNK, GLA_HEADS, GLA_HEAD_K)
    bcum = jnp.cumsum(glog, axis=2)
    b_end = bcum[:, :, -1]
    k_dec = kc * jnp.exp(b_end[:, :, None] - bcum)
    u = jnp.einsum('bnchk,bnchv->bnhkv', k_dec, vc)
    decay = jnp.exp(b_end)

    def step(state, inp):
        d, du = inp
        state = d[..., None] * state + du
        return state, state

    init = jnp.zeros((bsz, GLA_HEADS, GLA_HEAD_K, GLA_HEAD_V), f32)
    _, s_all = lax.scan(step, init, (jnp.moveaxis(decay, 1, 0), jnp.moveaxis(u, 1, 0)))
    s_all = jnp.moveaxis(s_all, 0, 1)
    o = jnp.einsum('bnchk,bnhkv->bnchv', qc, s_all).reshape(bsz, s, GLA_HEADS, GLA_HEAD_V)
    o = o * lax.rsqrt(jnp.mean(o * o, axis=-1, keepdims=True) + EPS) * norm_g.astype(f32)
    o = o.reshape(bsz, s, GLA_WIDTH).astype(q.dtype)
    return o * jax.nn.silu(r)


def setup_inputs(seed: int = 0) -> dict:
    key = jax.random.key(seed)
    ks = jax.random.split(key, 10)
    f32 = jnp.float32
    x = jax.random.normal(ks[0], (BATCH, SEQ, D_MODEL), f32)
    norm_g = 1.0 + 0.02 * jax.random.normal(ks[1], (DEPTH, D_MODEL), f32)
    w_in = jax.random.normal(ks[2], (DEPTH, D_MODEL, IN_COLS), f32) * D_MODEL ** -0.5
    conv_w = jax.random.normal(ks[3], (DEPTH, CONV_K, CONV_WIDTH), f32) * CONV_K ** -0.5
    conv_b = 0.02 * jax.random.normal(ks[4], (DEPTH, CONV_WIDTH), f32)
    gla_w_up = jax.random.normal(ks[5], (DEPTH, GLA_RANK, GLA_DK), f32) * GLA_RANK ** -0.5
    gla_b_gate = 2.0 + 0.1 * jax.random.normal(ks[6], (DEPTH, GLA_DK), f32)
    gla_norm_g = 1.0 + 0.02 * jax.random.normal(ks[7], (DEPTH, GLA_HEADS, GLA_HEAD_V), f32)
    w_out = jax.random.normal(ks[8], (DEPTH, MIX_WIDTH, D_MODEL), f32) * MIX_WIDTH ** -0.5
    final_g = 1.0 + 0.02 * jax.random.normal(ks[9], (D_MODEL,), f32)
    return {"x": x, "norm_g": norm_g, "w_in": w_in, "conv_w": conv_w, "conv_b": conv_b,
            "gla_w_up": gla_w_up, "gla_b_gate": gla_b_gate, "gla_norm_g": gla_norm_g,
            "w_out": w_out, "final_g": final_g}


def reference(x, norm_g, w_in, conv_w, conv_b, gla_w_up, gla_b_gate, gla_norm_g, w_out, final_g):
    for l in range(DEPTH):
        h = rms_norm(x, norm_g[l])
        proj = h @ w_in[l]
        ch, cb, cc, cz, gq, gk, gv, gr, gd = jnp.split(proj, SPLIT_IDX, axis=-1)
        y_conv = short_conv_branch(ch, cb, cc, cz, conv_w[l], conv_b[l])
        y_gla = gla_branch(gq, gk, gv, gr, gd, gla_w_up[l], gla_b_gate[l], gla_norm_g[l])
        y = jnp.concatenate([y_conv, y_gla], axis=-1)
        x = x + y @ w_out[l]
    return rms_norm(x, final_g)
```

```python
import numpy as np
import ml_dtypes
from contextlib import ExitStack
import concourse.bass as bass
import concourse.mybir as mybir
from concourse.bass_utils import run_bass_kernel_spmd

F32 = mybir.dt.float32
BF16 = mybir.dt.bfloat16
F32R = mybir.dt.float32r
ALU = mybir.AluOpType
AF = mybir.ActivationFunctionType

D = 2048
TOK = 1024
NT = 8
NDC = 16
INC = 7184
EPS = 1e-6
C_H, C_B, C_C, C_Z, C_Q, C_K, C_V, C_R, C_GD = 0, 1024, 2048, 3072, 4096, 4608, 5120, 6144, 7168
ENGS = ("pe", "act", "dve", "pool", "sp")


class Res:
    __slots__ = ("name", "writers", "readers", "overlaps")

    def __init__(self, name):
        self.name = name
        self.writers = []
        self.readers = []
        self.overlaps = []


def overlap(a_list, b_list):
    for a in a_list:
        for b in b_list:
            a.overlaps.append(b)
            b.overlaps.append(a)


class Op:
    __slots__ = ("eng", "fn", "deps", "dma", "idx", "sig", "nwait", "name")

    def __init__(self, eng, fn, deps, dma, idx, name):
        self.eng, self.fn, self.deps, self.dma, self.idx, self.name = eng, fn, deps, dma, idx, name
        self.sig = None
        self.nwait = 0


class Prog:
    def __init__(self, nc, n_dma_sems=8):
        self.nc = nc
        self.ops = []
        self.n_dma_sems = n_dma_sems

    def op(self, eng, fn, r=(), w=(), deps=(), dma=False, name=None, join=False):
        dl = {}
        for d in deps:
            if d is not None:
                dl[d.idx] = d
        for res in r:
            for x in [res] + res.overlaps:
                for wr in x.writers:
                    dl[wr.idx] = wr
        for res in w:
            for x in [res] + res.overlaps:
                if not (join and x is res):
                    for wr in x.writers:
                        dl[wr.idx] = wr
                for rd in x.readers:
                    dl[rd.idx] = rd
        dlist = [d for d in dl.values() if not (eng == "pe" and d.eng == "pe")]
        o = Op(eng, fn, dlist, dma, len(self.ops), name)
        for d in dlist:
            d.nwait += 1
        for res in r:
            res.readers.append(o)
        for res in w:
            if join:
                res.writers.append(o)
            else:
                res.writers = [o]
            res.readers = []
        self.ops.append(o)
        return o

    def dma(self, eng, out, in_, r=(), w=(), deps=(), name=None, join=False):
        return self.op(eng, lambda e: e.dma_start(out=out, in_=in_), r, w, deps, dma=True, name=name, join=join)

    def emit(self, final_waits):
        nc = self.nc
        with ExitStack() as st:
            csem = {e: st.enter_context(nc.semaphore("c_" + e)) for e in ENGS}
            dsem = {e: [st.enter_context(nc.semaphore("d_%s%d" % (e, i)))
                        for i in range(self.n_dma_sems)] for e in ("sp", "pool", "act")}
            ccount = {e: 0 for e in ENGS}
            drr = {e: 0 for e in dsem}
            duse = {e: [0] * self.n_dma_sems for e in dsem}
            dprev = {}
            self.op("sp", None, deps=list(final_waits), name="final")
            for o in self.ops:
                if o.dma:
                    k = drr[o.eng]
                    drr[o.eng] = (k + 1) % self.n_dma_sems
                    s = dsem[o.eng][k]
                    if duse[o.eng][k] > 0:
                        dprev[o.idx] = (s, 16 * duse[o.eng][k])
                    duse[o.eng][k] += 1
                    o.sig = (s, 16 * duse[o.eng][k], 16)
                elif o.nwait > 0:
                    ccount[o.eng] += 1
                    o.sig = (csem[o.eng], ccount[o.eng], 1)
            per = {e: [o for o in self.ops if o.eng == e] for e in ENGS}
            self.stats = {e: len(per[e]) for e in ENGS}
            self.stats["sem_max"] = dict(ccount)

            def run(engname, eng):
                seen = {}
                for o in per[engname]:
                    waits = []
                    if o.idx in dprev:
                        waits.append(dprev[o.idx])
                    for d in o.deps:
                        assert d.sig is not None, (o.name, d.name)
                        waits.append((d.sig[0], d.sig[1]))
                    for (s, v) in waits:
                        key = id(s)
                        if seen.get(key, 0) >= v:
                            continue
                        seen[key] = v
                        eng.wait_ge(s, v)
                    if o.fn is None:
                        continue
                    ins = o.fn(eng)
                    if o.sig is not None:
                        assert ins is not None, o.name
                        ins.then_inc(o.sig[0], o.sig[2])

            with nc.Block() as block:
                @block.tensor
                def _(e):
                    run("pe", e)

                @block.scalar
                def _(e):
                    run("act", e)

                @block.vector
                def _(e):
                    run("dve", e)

                @block.gpsimd
                def _(e):
                    run("pool", e)

                @block.sync
                def _(e):
                    run("sp", e)


def build(limit=5, debug=False):
    nc = bass.Bass("TRN2", target_bir_lowering=False)
    dram_in = lambda name, shape, dt=F32: nc.dram_tensor(name, shape, dt, kind="ExternalInput").ap()
    x_d = dram_in("x", [TOK + 2, D])
    win_d = dram_in("w_in", [D, INC])
    wout_d = dram_in("w_out", [D, D])
    gb_d = dram_in("norm_g1", [1, D])
    fg_d = dram_in("final_gb", [128, D])
    pfm_d = dram_in("pfm", [128, 48])
    wup_d = dram_in("w_up", [16, 512])
    bg_d = dram_in("b_gate", [1, 512])
    cst_d = dram_in("cst", [128, 128 + 2 + 128 + 6])
    idb_d = dram_in("identb", [128, 256], BF16)
    y_d = nc.dram_tensor("y", [TOK, D], F32, kind="ExternalOutput").ap()
    cc_in = nc.dram_tensor("cc_in", [128, 1028], F32)
    cc_out = nc.dram_tensor("cc_out", [4 * 128, 1028], F32)

    P = Prog(nc)
    st = ExitStack()
    with st:
        def sb(name, shape, dt):
            return st.enter_context(nc.sbuf_tensor("s_" + name, shape, dt))

        cst = sb("cst", [128, 264], F32)
        idb = sb("idb", [128, 256], BF16)
        pfm = sb("pfm", [128, 48], F32)
        wupr = sb("wupr", [33, 512], F32R)
        maskUr = sb("maskUr", [128, 128], F32R)
        stat = sb("stat", [128, 64], F32)
        dec = sb("dec", [128, 64], F32)
        bpre = sb("bpre", [128, 64], F32)
        dtot = sb("dtot", [128, 8], F32)
        hal = sb("hal", [128, 4], F32)
        regA = sb("regA", [128, 8192], F32)
        hT = sb("hT", [128, NDC, TOK], BF16)
        hTh = sb("hTh", [128, NDC, 2], BF16)
        NSL = 3
        wsl = [sb("wsl%d" % i, [128, NDC, 256], BF16) for i in range(NSL)]
        big = sb("big", [128, 16384], F32)
        yT = sb("yT", [128, 16, TOK], BF16)
        S = sb("S", [128, 4, 256], F32)
        Sbf = sb("Sbf", [128, 2, 4, 256], BF16)

        maskU = cst[:, 0:128]
        ind2 = cst[:, 128:130]
        ones_f = cst[:, 130:258]
        msk = cst[:, 258:261]
        omsk = cst[:, 261:264]
        ident = idb[:, 0:128]
        ones_b = idb[:, 128:256]
        cw = lambda j, g: pfm[:, j * 8 + g: j * 8 + g + 1]
        cbias = lambda g: pfm[:, 24 + g: 25 + g]
        gn = lambda hj: pfm[:, 32 + hj: 33 + hj]

        bigb = big[:].bitcast(BF16)
        kdec = bigb[:, 0:4096].rearrange("p (i c) -> p i c", i=NT)
        vsb = bigb[:, 4096:12288].rearrange("p (i c) -> p i c", i=NT)
        qT = bigb[:, 12288:16384].rearrange("p (h t) -> p h t", h=4)
        rT = bigb[:, 16384:24576].rearrange("p (g t) -> p g t", g=8)
        gdT = sb("gdTr", [33, 1024], F32R)
        lr = sb("lr", [128, 2, 512], F32R)
        stg = regA[:, 7168:8192]
        stgB = regA[:, 6144:6656]
        woutsb = bigb.rearrange("p (m c) -> p m c", m=16)
        Ab = regA[:].bitcast(BF16)
        kdecO = Ab[:, 12288:16384].rearrange("p (i c) -> p i c", i=NT)
        NXR = 4
        yTf = yT[:].rearrange("p m t -> p (m t)").bitcast(F32)
        xt0 = [yTf[:, k * 2048:(k + 1) * 2048] for k in range(NXR)]
        xs0 = [Ab[:, 0:2048], Ab[:, 2048:4096]]
        junk0 = Ab[:, 4096:6144]
        gbsb = regA[:, 3072:5120]
        g1 = big[:, 0:2048]
        e_sb = [regA[:, 5120:5632], regA[:, 5632:6144]]
        l_sb = [lr[:, 0, :], lr[:, 1, :]]
        wdec = regA[:, 0:4096].rearrange("p (i c) -> p i c", i=NT)
        gat = regA[:, 0:3 * 1028].rearrange("p (r c) -> p r c", r=3)
        Csb = regA[:, 0:1024]
        usb = regA[:, 1024:2050]
        ysb = regA[:, 2052:3076]
        szb = Ab[:, 6160:7184]
        def p2tmp(k):
            b = 4096 + k * 1024
            sq = Ab[:, 2 * b:2 * b + 512].rearrange("p (a t) -> p a t", a=8)
            rs = regA[:, b + 256:b + 512].rearrange("p (h t) -> p h t", h=4)
            y1 = regA[:, b + 512:b + 1024].rearrange("p (a t) -> p a t", a=8)
            return sq, rs, y1
        xt5 = [regA[:, 0:2048], regA[:, 2048:4096]]
        xo5 = [regA[:, 4096:6144], regA[:, 6144:8192]]

        SP, ACT, DVE, PE, POOL = "sp", "act", "dve", "pe", "pool"
        R = lambda n: Res(n)
        r_cst = R("cst")
        r_xt = [R("xt%d" % k) for k in range(NXR)]
        r_xs = [R("xs0"), R("xs1")]
        r_junk = R("junk")
        r_gb = R("gb")
        r_g1 = R("g1")
        r_st0 = [R("st0_%d" % i) for i in range(9)]
        r_st5 = [R("st5_%d" % i) for i in range(NT)]
        r_dm = R("dm")
        r_dummy = R("dummy")
        r_hT = [R("hT%d" % i) for i in range(NT)]
        r_hTh = R("hTh")
        r_w = [R("wsl%d" % i) for i in range(NSL)]
        r_kdec = [R("kdec%d" % i) for i in range(NT)]
        r_kz = R("kdec_zero")
        r_kdec2 = [[R("kdec%d_%d" % (i, b)) for b in range(2)] for i in range(NT)]
        r_v = [R("v%d" % i) for i in range(NT)]
        r_qT = [R("qT%d" % h) for h in range(4)]
        r_gdT = R("gdT")
        r_stg = R("stg")
        r_stgB = R("stgB")
        r_wupr = R("wupr")
        r_rT = [R("rT%d" % g) for g in range(8)]
        r_e = [R("e0"), R("e1")]
        r_l = [R("l0"), R("l1")]
        r_wdec = [R("wdec%d" % i) for i in range(NT)]
        r_dec = R("dec")
        r_gat = R("gat")
        r_S = [R("S%d" % h) for h in range(4)]
        r_Sbf = [[R("Sbf%d_%d" % (b, h)) for h in range(4)] for b in range(2)]
        r_yT = [[R("yT%d_%d" % (m, i)) for i in range(NT)] for m in range(16)]
        r_cv = [R("Csb"), R("usb"), R("ysb"), R("szb"), R("hal")]
        r_p2 = [[R("p2_%d_%d" % (k, j)) for j in range(3)] for k in range(2)]
        r_x5 = [R("x5_0"), R("x5_1")]
        r_xo = [R("xo0"), R("xo1")]
        r_xo5p = [R("xo5p%d" % c) for c in range(4)]
        r_wout = [R("wout%d" % c) for c in range(8)]
        r_fg = R("fg")
        r_cc = R("cc")
        p2all = r_p2[0] + r_p2[1]
        ph0A = r_xs + [r_junk, r_gb]
        ph1A = r_e + r_l + r_wdec + [r_gat]
        cvA = r_cv[0:4]
        ph5A = r_x5 + r_xo
        overlap(ph1A, ph0A)
        overlap([r_gat], r_wdec)
        overlap(cvA, ph0A + ph1A)
        overlap(p2all, ph0A + ph1A)
        overlap(ph5A, ph0A + ph1A + cvA + p2all)
        overlap(r_xt, [x for l in r_yT for x in l])
        overlap(r_wout, r_kdec + r_v + r_qT + [r_gdT] + r_rT)
        overlap([r_fg], r_w)
        overlap([r_stg, r_stgB], ph5A)
        overlap(r_xo5p, [r_xo[1]])
        overlap([r_g1], r_kdec + r_wout)
        allk2 = [x for l in r_kdec2 for x in l]
        overlap(allk2 + r_kdec + [r_kz], [r_stg, r_stgB] + ph5A)
        overlap(allk2, r_wout + [r_g1])

        pbank = [st.enter_context(nc.psum_tensor("pb%d" % i, [128, 512], F32)) for i in range(8)]
        r_pb = [R("pb%d" % i) for i in range(8)]
        r_mbh = R("mb_halo")
        r_mbs = R("mb_ss")

        stores = []
        dbgs = []

        def dump(name, ap, shape, dt, res):
            t = nc.dram_tensor("dbg_" + name, shape, dt, kind="ExternalOutput").ap()
            dbgs.append((t, ap, res))

        def body():
            P.op(DVE, lambda e: e.memset(stat[:, 56:64], 1.0), w=[r_dummy])
            def load_consts():
                P.dma(SP, cst[:], cst_d, w=[r_cst])
                P.dma(SP, idb[:], idb_d, w=[r_cst], join=True)
                P.dma(SP, pfm[:], pfm_d, w=[r_cst], join=True)

            wcols = lambda c0, n: win_d[:, c0:c0 + n]
            wconv = win_d[:, 0:4096].rearrange("(dc p) (s t g j) -> p dc s t g j", p=128, s=2, t=2, g=8, j=128)
            blk256 = lambda c0: [(wcols(c0, 256).rearrange("(dc p) c -> p dc c", p=128), 0, 256)]
            wplan = [blk256(C_V)]
            wplan += [[(wcols(C_GD, 16).rearrange("(dc p) c -> p dc c", p=128), 0, 16)]]
            wplan += [blk256(C_V + b * 256) for b in range(1, 4)]
            wplan += [blk256(C_K + b * 256) for b in range(2)]
            wplan += [blk256(C_Q + b * 256) for b in range(2)]
            wplan += [blk256(C_R + b * 256) for b in range(4)]
            for g in range(8):
                for t in range(2):
                    wplan.append([(wconv[:, :, sg, t, g, :], sg * 128, 128) for sg in range(2)])
            wst = {"loaded": 0, "used": 0}

            def w_emit(deps=()):
                n = wst["loaded"]
                if n >= len(wplan):
                    return
                wst["loaded"] += 1
                s = n % NSL
                for k, (src, c0, nn) in enumerate(wplan[n]):
                    P.dma(POOL, wsl[s][:, :, c0:c0 + nn], src, w=[r_w[s]], join=(k > 0), deps=deps)

            def wload():
                n = wst["used"]
                wst["used"] += 1
                assert n < wst["loaded"]
                return n % NSL

            def wdone(k=1):
                for _ in range(k):
                    w_emit()

            def x_load(i):
                if i < NT:
                    return P.dma(SP, xt0[i % NXR], x_d[2 + i * 128: 2 + (i + 1) * 128, :], w=[r_xt[i % NXR]])
                return P.dma(SP, xt0[i % NXR][0:2, :], x_d[0:2, :], w=[r_xt[i % NXR]])

            def st1(i, npart):
                xk = i % NXR
                sc = stat[0:npart, i:i + 1]
                P.op(ACT, lambda e: e.activation(out=junk0[0:npart, :], in_=xt0[xk][0:npart, :], func=AF.Square,
                                                 accum_out=sc), r=[r_xt[xk]], w=[r_junk, r_st0[i]])
                P.op(ACT, lambda e: e.activation(out=sc, in_=sc, func=AF.Ln, scale=1.0 / D, bias=EPS),
                     r=[r_st0[i]], w=[r_st0[i]])
                P.op(ACT, lambda e: e.activation(out=sc, in_=sc, func=AF.Exp, scale=-0.5), r=[r_st0[i]], w=[r_st0[i]])

            def st2(i, npart):
                b = i % 2
                xk = i % NXR
                sc = stat[0:npart, i:i + 1]
                P.op(DVE, lambda e: e.scalar_tensor_tensor(out=xs0[b][0:npart, :], in0=xt0[xk][0:npart, :], scalar=sc,
                                                           in1=gbsb[0:npart, :], op0=ALU.mult, op1=ALU.mult),
                     r=[r_xt[xk], r_st0[i], r_gb], w=[r_xs[b]])

            def st3(i):
                if i == NT:
                    pvh = pbank[0][:].bitcast(BF16)

                    def trh(e):
                        ins = None
                        for dc in range(16):
                            ins = e.transpose(out=pvh[:, dc * 2:dc * 2 + 2], in_=xs0[0][0:2, dc * 128:(dc + 1) * 128],
                                              identity=ident[0:2, 0:2])
                        return ins
                    P.op(PE, trh, r=[r_xs[0], r_cst], w=[r_pb[0]])
                    P.op(DVE, lambda e: e.tensor_copy(out=hTh[:], in_=pvh[:, 0:32].rearrange("p (d t) -> p d t", d=16)),
                         r=[r_pb[0]], w=[r_hTh])
                    return
                for half in range(2):
                    bk = (2 * i + half) % 4
                    pv = pbank[bk][:].bitcast(BF16)

                    def tr(e, half=half, pv=pv):
                        ins = None
                        for j in range(8):
                            dc = half * 8 + j
                            ins = e.transpose(out=pv[:, j * 128:(j + 1) * 128],
                                              in_=xs0[i % 2][:, dc * 128:(dc + 1) * 128], identity=ident)
                        return ins
                    P.op(PE, tr, r=[r_xs[i % 2], r_cst], w=[r_pb[bk]])
                    dst = hT[:, half * 8:(half + 1) * 8, i * 128:(i + 1) * 128]
                    srcv = pv.rearrange("p (j t) -> p j t", j=8)
                    if half == 0:
                        P.op(ACT, lambda e, dst=dst, srcv=srcv: e.activation(out=dst, in_=srcv, func=AF.Copy),
                             r=[r_pb[bk]], w=[r_hT[i]])
                    else:
                        P.op(DVE, lambda e, dst=dst, srcv=srcv: e.tensor_copy(out=dst, in_=srcv),
                             r=[r_pb[bk]], w=[r_hT[i]], join=True)

            ring = {"n": 0, "k": 6}

            def nb():
                b = ring["n"] % ring["k"]
                ring["n"] += 1
                return b

            def featmajor(s, col0, halo_bank=None, halo_col=0, mid=None):
                ba, bb = nb(), nb()
                first = 4 if (mid is not None and halo_bank is not None) else 0
                last = (first - 1) % NDC

                def mm(e, dcs, halo):
                    ins = None
                    for dc in dcs:
                        lw = wsl[s][:, dc, col0:col0 + 128]
                        ins = e.matmul(pbank[ba][:], lhsT=lw, rhs=hT[:, dc, 0:512], start=(dc == 0), stop=(dc == NDC - 1))
                        ins = e.matmul(pbank[bb][:], lhsT=lw, rhs=hT[:, dc, 512:1024], start=(dc == 0), stop=(dc == NDC - 1))
                        if halo:
                            ins = e.matmul(pbank[halo_bank][:, halo_col:halo_col + 2], lhsT=lw, rhs=hTh[:, dc, :],
                                           start=(dc == first), stop=(dc == last))
                    return ins

                def halo_only(e, dcs):
                    ins = None
                    for dc in dcs:
                        ins = e.matmul(pbank[halo_bank][:, halo_col:halo_col + 2], lhsT=wsl[s][:, dc, col0:col0 + 128],
                                       rhs=hTh[:, dc, :], start=(dc == first), stop=(dc == last))
                    return ins
                hb = halo_bank is not None
                wr = [r_pb[ba], r_pb[bb]]
                rd = [r_w[s], r_hTh] + r_hT
                if mid is None:
                    P.op(PE, lambda e: mm(e, range(NDC), hb), r=rd, w=wr + ([r_pb[halo_bank]] if hb else []))
                else:
                    P.op(PE, lambda e: mm(e, range(0, 4), False), r=rd, w=wr)
                    mid()
                    if hb:
                        P.op(PE, lambda e: (mm(e, range(4, NDC), True), halo_only(e, range(0, 4)))[1], r=rd,
                             w=wr + [r_pb[halo_bank]])
                    else:
                        P.op(PE, lambda e: mm(e, range(4, NDC), False), r=rd, w=wr)
                return ba, bb

            def tok_unit(s, i, evac, bk=None):
                if bk is None:
                    bk = nb()

                def mm(e):
                    ins = None
                    for dc in range(NDC):
                        ins = e.matmul(pbank[bk][:, 0:256], lhsT=hT[:, dc, i * 128:(i + 1) * 128],
                                       rhs=wsl[s][:, dc, :], start=(dc == 0), stop=(dc == NDC - 1))
                    return ins
                P.op(PE, mm, r=[r_w[s], r_hT[i]], w=[r_pb[bk]])
                evac(i, bk)

            P.dma(SP, g1[0:1, :], gb_d, w=[r_g1])
            load_consts()
            x_load(0)
            P.op(DVE, lambda e: e.memset(stg[0:64, :], 0.0), w=[r_stg])
            P.op(DVE, lambda e: e.memset(stg[32:33, :], 1.0), w=[r_stg])
            P.op(DVE, lambda e: e.memset(stgB[0:64, :], 0.0), w=[r_stgB])
            for c in range(4):
                P.op(PE, lambda e, c=c: e.matmul(pbank[4 + c][:], lhsT=ones_f[0:1, :], rhs=g1[0:1, c * 512:(c + 1) * 512],
                                                 start=True, stop=True), r=[r_cst, r_g1], w=[r_pb[4 + c]])
                P.op(DVE, lambda e, c=c: e.tensor_copy(out=gbsb[:, c * 512:(c + 1) * 512], in_=pbank[4 + c][:]),
                     r=[r_pb[4 + c]], w=[r_gb], join=(c > 0))
            x_load(1)
            x3op = x_load(2)
            x_load(3)
            P.dma(SP, stgB[0:16, :], wup_d, w=[r_stgB])
            P.dma(SP, stgB[32:33, :], bg_d, w=[r_stgB], join=True)
            for _ in range(NSL):
                w_emit(deps=[x3op])

            def make_evac_v(blk):
                def evac_v(i, bk):
                    if (i + blk) % 2 == 0:
                        P.op(ACT, lambda e: e.activation(out=vsb[:, i, blk * 256:(blk + 1) * 256],
                                                         in_=pbank[bk][:, 0:256], func=AF.Copy),
                             r=[r_pb[bk]], w=[r_v[i]])
                    else:
                        P.op(DVE, lambda e: e.tensor_copy(out=vsb[:, i, blk * 256:(blk + 1) * 256],
                                                          in_=pbank[bk][:, 0:256]), r=[r_pb[bk]], w=[r_v[i]])
                return evac_v

            s_v0 = wload()
            npt = lambda i: 128 if i < NT else 2
            for t in range(NT + 8):
                if t <= NT:
                    st1(t, npt(t))
                if 0 <= t - 1 <= NT:
                    st2(t - 1, npt(t - 1))
                if 4 <= t + 3 <= NT:
                    x_load(t + 3)
                if 0 <= t - 2 <= NT:
                    st3(t - 2)
                if 0 <= t - 8 < NT:
                    tok_unit(s_v0, t - 8, make_evac_v(0), bk=4 + (t - 8) % 4)
            wdone()
            s_gd = wload()
            if limit < 1:
                return

            P.op(ACT, lambda e: e.activation(out=gdT[0:33, :], in_=stg[0:33, :], func=AF.Copy), r=[r_stg], w=[r_gdT])
            P.op(ACT, lambda e: e.activation(out=wupr[0:33, :], in_=stgB[0:33, :], func=AF.Copy), r=[r_stgB], w=[r_wupr])
            P.op(ACT, lambda e: e.activation(out=maskUr[:], in_=maskU, func=AF.Copy), r=[r_cst], w=[r_wupr], join=True)
            P.op(POOL, lambda e: e.memset(kdecO[0:64, :, :], 0.0), r=[r_stg, r_stgB], w=[r_kz] + r_kdec)
            b0, b1 = nb(), nb()

            def gdmm(e, s=s_gd):
                ins = None
                for dc in range(NDC):
                    for tb, bk in ((0, b0), (1, b1)):
                        ins = e.matmul(pbank[bk][0:16, :], lhsT=wsl[s][:, dc, 0:16], rhs=hT[:, dc, tb * 512:(tb + 1) * 512],
                                       start=(dc == 0), stop=(dc == NDC - 1))
                return ins
            P.op(PE, gdmm, r=[r_w[s_gd]] + r_hT, w=[r_pb[b0], r_pb[b1]])
            wdone()
            P.op(ACT, lambda e: e.activation(out=gdT[0:16, 0:512], in_=pbank[b0][0:16, :], func=AF.Copy),
                 r=[r_pb[b0]], w=[r_gdT])
            P.op(ACT, lambda e: e.activation(out=gdT[0:16, 512:1024], in_=pbank[b1][0:16, :], func=AF.Copy),
                 r=[r_pb[b1]], w=[r_gdT])

            def gen_v():
                for blk in range(1, 4):
                    s = wload()
                    for i in range(NT):
                        tok_unit(s, i, make_evac_v(blk))
                        if i == NT - 1:
                            wdone()
                        yield
            gv = gen_v()
            next(gv)
            next(gv)

            for i in range(NT):
                j = i % 2
                bx = nb()
                P.op(PE, lambda e, i=i, bx=bx: e.matmul(pbank[bx][:], lhsT=gdT[0:33, i * 128:(i + 1) * 128],
                                                        rhs=wupr[0:33, :], start=True, stop=True),
                     r=[r_gdT, r_wupr], w=[r_pb[bx]])
                P.op(ACT, lambda e, j=j, bx=bx: e.activation(out=e_sb[j], in_=pbank[bx][:], func=AF.Exp, scale=-1.0),
                     r=[r_pb[bx]], w=[r_e[j]])
                P.op(ACT, lambda e, j=j: e.activation(out=l_sb[j], in_=e_sb[j], func=AF.Ln, bias=1.0),
                     r=[r_e[j]], w=[r_l[j]])
                next(gv)
                next(gv)
                brv, bbe = nb(), nb()
                P.op(PE, lambda e, j=j, brv=brv: e.matmul(pbank[brv][:], lhsT=maskUr[:], rhs=l_sb[j],
                                                          start=True, stop=True),
                     r=[r_l[j], r_wupr], w=[r_pb[brv]])
                P.op(ACT, lambda e, i=i, brv=brv: e.activation(out=wdec[:, i, :], in_=pbank[brv][:], func=AF.Exp,
                                                               scale=-1.0 / 16.0),
                     r=[r_pb[brv]], w=[r_wdec[i]])

                def bem(e, j=j, bbe=bbe):
                    ins = None
                    for h in range(4):
                        ins = e.matmul(pbank[bbe][:, h * 2:h * 2 + 2], lhsT=l_sb[j][:, h * 128:(h + 1) * 128].bitcast(F32), rhs=ind2,
                                       start=True, stop=True)
                    return ins
                P.op(PE, bem, r=[r_l[j], r_cst], w=[r_pb[bbe]])
                P.op(DVE, lambda e, i=i, bbe=bbe: e.tensor_copy(out=bpre[:, i * 8:(i + 1) * 8], in_=pbank[bbe][:, 0:8]),
                     r=[r_pb[bbe]], w=[r_dec])
                if i < 6:
                    next(gv)
            for _ in gv:
                pass
            P.op(ACT, lambda e: e.activation(out=dec[:], in_=bpre[:], func=AF.Exp, scale=-1.0 / 16.0), r=[r_dec], w=[r_dec])
            P.op(DVE, lambda e: e.tensor_reduce(out=dtot[:, 4:8], in_=bpre[:].rearrange("p (i h s) -> p h i s", i=8, h=4),
                                                axis=mybir.AxisListType.XY, op=ALU.add), r=[r_dec], w=[r_dec])
            P.op(ACT, lambda e: e.activation(out=dtot[:, 0:4], in_=dtot[:, 4:8], func=AF.Exp, scale=-1.0 / 16.0),
                 r=[r_dec], w=[r_dec])

            UB1 = [6, 7]

            def p1_chunk(n):
                i, s_ = n // 2, n % 2
                ps0, ps1 = s_ * 64, s_ * 64 + 64
                for h in range(4):
                    ub, uo = UB1[h // 2], (h % 2) * 256
                    P.op(PE, lambda e, i=i, h=h, ub=ub, uo=uo: e.matmul(
                        pbank[ub][:, uo:uo + 256], lhsT=(kdec, kdecO)[s_][:, i, h * 128:(h + 1) * 128],
                        rhs=vsb[:, i, h * 256:(h + 1) * 256], start=True, stop=True),
                        r=[r_kdec2[i][h // 2], r_v[i], r_kz], w=[r_pb[ub]])
                for h in range(4):
                    ub, uo = UB1[h // 2], (h % 2) * 256
                    P.op(DVE, lambda e, i=i, h=h, ub=ub, uo=uo: e.scalar_tensor_tensor(
                        out=S[:, h, :], in0=S[:, h, :], scalar=dec[:, i * 8 + h * 2 + s_: i * 8 + h * 2 + s_ + 1],
                        in1=pbank[ub][:, uo:uo + 256], op0=ALU.mult, op1=ALU.add),
                        r=[r_S[h], r_dec, r_pb[ub]], w=[r_S[h]])

            ring["n"] = 0
            P.op(DVE, lambda e: e.memset(S[:], 0.0), w=r_S)
            s_k = [wload(), wload()]
            for i in range(NT):
                for blk in range(2):
                    def evac_k(i, bk, blk=blk):
                        cs = slice(blk * 256, (blk + 1) * 256)
                        rk = r_kdec2[i][blk]
                        P.op(DVE, lambda e: e.tensor_tensor(out=kdec[:, i, cs], in0=pbank[bk][:, 0:256],
                                                            in1=wdec[:, i, cs], op=ALU.mult),
                             r=[r_pb[bk], r_wdec[i]], w=[rk])
                        P.op(POOL, lambda e: e.tensor_copy(out=kdecO[64:128, i, cs], in_=kdec[64:128, i, cs]),
                             r=[rk, r_kz], w=[rk])
                        P.op(POOL, lambda e: e.memset(kdec[64:128, i, cs], 0.0), w=[rk])
                    tok_unit(s_k[blk], i, evac_k)
                    if i >= 1:
                        p1_chunk(2 * (i - 1) + blk)
                if i == NT - 1:
                    wdone(2)

            def q_group(blk, hh, s):
                h = blk * 2 + hh
                ba, bb = featmajor(s, hh * 128)
                for tb, bk in ((0, ba), (1, bb)):
                    P.op(ACT, lambda e, h=h, tb=tb, bk=bk: e.activation(out=qT[:, h, tb * 512:(tb + 1) * 512],
                                                                        in_=pbank[bk][:], func=AF.Copy,
                                                                        scale=float(128 ** -0.5)),
                         r=[r_pb[bk]], w=[r_qT[h]])
            s_q0 = wload()
            q_group(0, 0, s_q0)
            p1_chunk(14)
            p1_chunk(15)
            if limit >= 2:
                P.dma(SP, cc_in.ap()[:, 0:1024], S[:].rearrange("p h v -> p (h v)"), r=r_S, w=[r_cc])
                P.dma(SP, cc_in.ap()[:, 1024:1028], dtot[:, 0:4], r=[r_dec], w=[r_cc])
            q_group(0, 1, s_q0)
            wdone()
            s_q1 = wload()
            if limit >= 2:
                P.op(POOL, lambda e: e.collective_compute("AllGather", ALU.bypass,
                                                          replica_groups=[[0, 1, 2, 3], [4, 5, 6, 7]],
                                                          ins=[cc_in.ap().opt()], outs=[cc_out.ap().opt()], dma_qos="P2"),
                     r=[r_cc], w=[r_cc])
            q_group(1, 0, s_q1)
            q_group(1, 1, s_q1)
            wdone()
            if limit < 2:
                return

            def combine():
                P.dma(SP, gat, cc_out.ap()[0:384, :].rearrange("(r p) c -> p r c", p=128), r=[r_cc], w=[r_gat])
                P.op(DVE, lambda e: e.memset(S[:], 0.0), w=r_S)
                dm = stat[:, 16:28]
                for r_ in range(3):
                    P.op(DVE, lambda e, r_=r_: e.tensor_scalar(out=dm[:, r_ * 4:(r_ + 1) * 4], in0=gat[:, r_, 1024:1028],
                                                               scalar1=msk[:, r_:r_ + 1], scalar2=omsk[:, r_:r_ + 1],
                                                               op0=ALU.mult, op1=ALU.add), r=[r_gat, r_cst], w=[r_dm])
                    P.op(DVE, lambda e, r_=r_: e.tensor_scalar(out=gat[:, r_, 0:1024], in0=gat[:, r_, 0:1024],
                                                               scalar1=msk[:, r_:r_ + 1], scalar2=None, op0=ALU.mult),
                         r=[r_gat, r_cst], w=[r_gat])
                    for h in range(4):
                        P.op(DVE, lambda e, r_=r_, h=h: e.scalar_tensor_tensor(
                            out=S[:, h, :], in0=S[:, h, :], scalar=dm[:, r_ * 4 + h:r_ * 4 + h + 1],
                            in1=gat[:, r_, h * 256:(h + 1) * 256], op0=ALU.mult, op1=ALU.add),
                            r=[r_gat, r_dm, r_S[h]], w=[r_S[h]])

            for blk in range(4):
                s = wload()
                for hh in range(2):
                    g = blk * 2 + hh
                    ba, bb = featmajor(s, hh * 128)
                    for tb, bk in ((0, ba), (1, bb)):
                        P.op(ACT, lambda e, g=g, tb=tb, bk=bk: e.activation(out=rT[:, g, tb * 512:(tb + 1) * 512],
                                                                            in_=pbank[bk][:], func=AF.Silu),
                             r=[r_pb[bk]], w=[r_rT[g]])
                    P.op(DVE, lambda e, g=g: e.tensor_scalar(out=rT[:, g, :], in0=rT[:, g, :], scalar1=gn(g), scalar2=None,
                                                             op0=ALU.mult), r=[r_rT[g], r_cst], w=[r_rT[g]])
                    if hh == 1:
                        wdone()
                        if blk == 2:
                            combine()

            if debug:
                dump("Sin", S[:], [128, 4, 256], F32, r_S)
                for (t, ap, res) in dbgs:
                    stores.append(P.dma(SP, t, ap, r=res))
                del dbgs[:]
            if limit < 3:
                return

            MB = 4
            UB2 = 5
            OB = [6, 7]

            def p2_A(n, hp):
                i, s_ = n // 2, n % 2
                ps0, ps1 = s_ * 64, s_ * 64 + 64
                bf = n % 2
                for hh in range(2):
                    h = hp * 2 + hh
                    uo = hh * 256
                    P.op(PE, lambda e, i=i, h=h, uo=uo: e.matmul(
                        pbank[UB2][:, uo:uo + 256], lhsT=(kdec, kdecO)[s_][:, i, h * 128:(h + 1) * 128],
                        rhs=vsb[:, i, h * 256:(h + 1) * 256], start=True, stop=True),
                        r=[r_kdec2[i][h // 2], r_v[i], r_kz], w=[r_pb[UB2]])
                for hh in range(2):
                    h = hp * 2 + hh
                    uo = hh * 256
                    P.op(DVE, lambda e, i=i, h=h, uo=uo: e.scalar_tensor_tensor(
                        out=S[:, h, :], in0=S[:, h, :], scalar=dec[:, i * 8 + h * 2 + s_: i * 8 + h * 2 + s_ + 1],
                        in1=pbank[UB2][:, uo:uo + 256], op0=ALU.mult, op1=ALU.add),
                        r=[r_S[h], r_dec, r_pb[UB2]], w=[r_S[h]])
                    P.op(ACT, lambda e, h=h, bf=bf: e.activation(out=Sbf[:, bf, h, :], in_=S[:, h, :], func=AF.Copy),
                         r=[r_S[h]], w=[r_Sbf[bf][h]])

            def p2_B(n):
                bf = n % 2
                ob = OB[n % 2]

                def om(e):
                    ins = None
                    for h in range(4):
                        for j in range(2):
                            hj = h * 2 + j
                            ins = e.matmul(pbank[ob][:, hj * 64:(hj + 1) * 64], lhsT=Sbf[:, bf, h, j * 128:(j + 1) * 128],
                                           rhs=qT[:, h, n * 64:(n + 1) * 64], start=True, stop=True)
                    return ins
                P.op(PE, om, r=r_Sbf[bf] + r_qT, w=[r_pb[ob]])
                sq, rs, y1 = p2tmp(n % 2)
                rp = r_p2[n % 2]
                P.op(ACT, lambda e: e.activation(out=sq, in_=pbank[ob][:].rearrange("p (a t) -> p a t", a=8),
                                                 func=AF.Square), r=[r_pb[ob]], w=[rp[0]])

            def p2_C(n):
                ob = OB[n % 2]
                sq, rs, y1 = p2tmp(n % 2)
                rp = r_p2[n % 2]

                def ssm(e):
                    ins = None
                    for h in range(4):
                        for j in range(2):
                            ins = e.matmul(pbank[MB][:, 256 + h * 64:256 + (h + 1) * 64], lhsT=ones_b, rhs=sq[:, h * 2 + j, :],
                                           start=(j == 0), stop=(j == 1))
                    return ins
                P.op(PE, ssm, r=[rp[0], r_cst], w=[r_pb[MB]])
                P.op(ACT, lambda e: e.activation(out=rs, in_=pbank[MB][:, 256:512].rearrange("p (h t) -> p h t", h=4),
                                                 func=AF.Ln, scale=1.0 / 256.0, bias=EPS), r=[r_pb[MB]], w=[rp[1]])
                P.op(ACT, lambda e: e.activation(out=rs, in_=rs, func=AF.Exp, scale=-0.5), r=[rp[1]], w=[rp[1]])
                P.op(DVE, lambda e: e.tensor_tensor(
                    out=y1.rearrange("p (h j) t -> p h j t", h=4),
                    in0=pbank[ob][:].rearrange("p (h j t) -> p h j t", h=4, j=2),
                    in1=rs.unsqueeze(2).to_broadcast([128, 4, 2, 64]), op=ALU.mult),
                    r=[r_pb[ob], rp[1]], w=[rp[2]])
                i = n // 2
                P.op(DVE, lambda e: e.tensor_tensor(out=yT[:, 8:16, n * 64:(n + 1) * 64], in0=y1,
                                                    in1=rT[:, :, n * 64:(n + 1) * 64], op=ALU.mult),
                     r=[rp[2]] + r_rT, w=[r_yT[m][i] for m in range(8, 16)])

            def p2_pre(k):
                if 0 <= k - 2 < 16:
                    p2_B(k - 2)

            def p2_mid(k):
                if 0 <= k < 16:
                    p2_A(k, 0)

            def p2_post(k):
                if 0 <= k < 16:
                    p2_A(k, 1)
                if 0 <= k - 2 < 16:
                    p2_C(k - 2)

            ring["n"] = 0
            ring["k"] = 4
            unit = {"k": 0}

            P2LAG = 1

            def side():
                p2_pre(unit["k"] - P2LAG)

            def mid():
                p2_mid(unit["k"] - P2LAG)

            def post():
                p2_post(unit["k"] - P2LAG)
                unit["k"] += 1

            WSCHED = {5: (2, 3, 4), 6: (5, 6, 7)}
            WSCHED_END = {4: (0, 1)}
            for g in range(8):
                s1 = wload()
                hc = g * 4
                side()
                ca, cbk = featmajor(s1, 128, MB, hc, mid=mid)
                P.op(ACT, lambda e, hc=hc: e.activation(out=hal[:, 0:2], in_=pbank[MB][:, hc:hc + 2], func=AF.Copy),
                     r=[r_pb[MB]], w=[r_cv[4]])
                post()
                P.op(ACT, lambda e, ca=ca: e.activation(out=Csb[:, 0:512], in_=pbank[ca][:], func=AF.Copy),
                     r=[r_pb[ca]], w=[r_cv[0]])
                P.op(ACT, lambda e, cbk=cbk: e.activation(out=Csb[:, 512:1024], in_=pbank[cbk][:], func=AF.Copy),
                     r=[r_pb[cbk]], w=[r_cv[0]])
                side()
                ha, hb_ = featmajor(s1, 0, MB, hc + 2, mid=mid)
                P.op(ACT, lambda e, hc=hc: e.activation(out=hal[:, 2:4], in_=pbank[MB][:, hc + 2:hc + 4], func=AF.Copy),
                     r=[r_pb[MB]], w=[r_cv[4]], join=True)
                post()
                P.op(DVE, lambda e: e.tensor_tensor(out=usb[:, 0:2], in0=hal[:, 2:4], in1=hal[:, 0:2], op=ALU.mult),
                     r=[r_cv[4]], w=[r_cv[1]])
                P.op(DVE, lambda e, ha=ha: e.tensor_tensor(out=usb[:, 2:514], in0=pbank[ha][:], in1=Csb[:, 0:512],
                                                           op=ALU.mult), r=[r_pb[ha], r_cv[0]], w=[r_cv[1]])
                P.op(DVE, lambda e, hb_=hb_: e.tensor_tensor(out=usb[:, 514:1026], in0=pbank[hb_][:], in1=Csb[:, 512:1024],
                                                             op=ALU.mult), r=[r_pb[hb_], r_cv[0]], w=[r_cv[1]])
                P.op(DVE, lambda e, g=g: e.tensor_scalar(out=ysb, in0=usb[:, 2:1026], scalar1=cw(2, g),
                                                         scalar2=cbias(g), op0=ALU.mult, op1=ALU.add),
                     r=[r_cv[1], r_cst], w=[r_cv[2]])
                P.op(DVE, lambda e, g=g: e.scalar_tensor_tensor(out=ysb, in0=usb[:, 1:1025], scalar=cw(1, g),
                                                                in1=ysb, op0=ALU.mult, op1=ALU.add),
                     r=[r_cv[1], r_cv[2], r_cst], w=[r_cv[2]])
                P.op(DVE, lambda e, g=g: e.scalar_tensor_tensor(out=ysb, in0=usb[:, 0:1024], scalar=cw(0, g),
                                                                in1=ysb, op0=ALU.mult, op1=ALU.add),
                     r=[r_cv[1], r_cv[2], r_cst], w=[r_cv[2]])
                wdone()
                if limit >= 5 and g in WSCHED:
                    for cb in WSCHED[g]:
                        P.dma(POOL, woutsb[:, :, cb * 256:(cb + 1) * 256],
                              wout_d[:, cb * 256:(cb + 1) * 256].rearrange("(m p) c -> p m c", p=128), w=[r_wout[cb]])
                s2 = wload()
                side()
                Ba, Bb = featmajor(s2, 0, mid=mid)
                post()
                for tb, bk in ((0, Ba), (1, Bb)):
                    P.op(DVE, lambda e, tb=tb, bk=bk: e.tensor_tensor(out=ysb[:, tb * 512:(tb + 1) * 512], in0=pbank[bk][:],
                                                                      in1=ysb[:, tb * 512:(tb + 1) * 512], op=ALU.mult),
                         r=[r_pb[bk], r_cv[2]], w=[r_cv[2]])
                side()
                za, zb = featmajor(s2, 128, mid=mid)
                post()
                for tb, bk in ((0, za), (1, zb)):
                    P.op(ACT, lambda e, tb=tb, bk=bk: e.activation(out=szb[:, tb * 512:(tb + 1) * 512], in_=pbank[bk][:],
                                                                   func=AF.Silu), r=[r_pb[bk]], w=[r_cv[3]])
                if unit["k"] < 19 + P2LAG:
                    P.op(ACT, lambda e: e.activation(out=stat[:, 61:62], in_=stat[:, 60:61], func=AF.Ln, bias=1.0),
                         w=[r_dummy])
                P.op(DVE, lambda e, g=g: e.tensor_tensor(out=yT[:, g, :], in0=ysb, in1=szb, op=ALU.mult),
                     r=[r_cv[2], r_cv[3]], w=[r_yT[g][i] for i in range(NT)])
                wdone()
                if limit >= 5 and g in WSCHED_END:
                    for cb in WSCHED_END[g]:
                        P.dma(POOL, woutsb[:, :, cb * 256:(cb + 1) * 256],
                              wout_d[:, cb * 256:(cb + 1) * 256].rearrange("(m p) c -> p m c", p=128), w=[r_wout[cb]])
            while unit["k"] < 18 + P2LAG:
                side()
                mid()
                post()
            if limit < 5:
                return

            fgv = wsl[0][:].rearrange("p a b -> p (a b)").bitcast(F32)
            P.dma(SP, fgv, fg_d, w=[r_fg])

            def x5_load(i):
                P.dma(SP, xt5[i % 2], x_d[2 + i * 128: 2 + (i + 1) * 128, :], w=[r_x5[i % 2]])
            x5_load(0)
            x5_load(1)
            for i in range(NT):
                b = i % 2
                base = 0 if i % 2 == 0 else 4
                for c4 in range(4):
                    bk = base + c4

                    def om(e, i=i, c4=c4, bk=bk, ms=()):
                        ins = None
                        for m in ms:
                            ins = e.matmul(pbank[bk][:], lhsT=yT[:, m, i * 128:(i + 1) * 128],
                                           rhs=woutsb[:, m, c4 * 512:(c4 + 1) * 512], start=(m == 8), stop=(m == 7))
                        return ins
                    ms1, ms2 = list(range(8, 16)), list(range(0, 8))
                    P.op(PE, lambda e, om=om, ms1=ms1: om(e, ms=ms1),
                         r=[r_yT[m][i] for m in ms1] + [r_wout[2 * c4], r_wout[2 * c4 + 1]], w=[r_pb[bk]])
                    P.op(PE, lambda e, om=om, ms2=ms2: om(e, ms=ms2),
                         r=[r_yT[m][i] for m in ms2] + [r_wout[2 * c4], r_wout[2 * c4 + 1]], w=[r_pb[bk]])
                    P.op(DVE, lambda e, b=b, c4=c4, bk=bk: e.tensor_tensor(out=xo5[b][:, c4 * 512:(c4 + 1) * 512],
                                                                           in0=pbank[bk][:],
                                                                           in1=xt5[b][:, c4 * 512:(c4 + 1) * 512], op=ALU.add),
                         r=[r_pb[bk], r_x5[b]], w=[r_xo[b]], join=(c4 > 0))
                    if i == NT - 1:
                        cs = slice(c4 * 512, (c4 + 1) * 512)
                        P.op(ACT, lambda e, b=b, c4=c4, cs=cs: e.activation(out=xt5[b][:, cs], in_=xo5[b][:, cs], func=AF.Square,
                                                                            accum_out=stat[:, 44 + c4:45 + c4]),
                             r=[r_xo[b]], w=[r_x5[b], r_st5[i]], join=(c4 > 0))
                        P.op(DVE, lambda e, b=b, cs=cs: e.tensor_tensor(out=xo5[b][:, cs], in0=xo5[b][:, cs], in1=fgv[:, cs],
                                                                        op=ALU.mult),
                             r=[r_xo[b], r_fg], w=[r_xo5p[c4]])
                sc = stat[:, 32 + i:33 + i]
                if i == NT - 1:
                    P.op(DVE, lambda e, sc=sc: e.tensor_reduce(out=sc, in_=stat[:, 44:48], axis=mybir.AxisListType.X, op=ALU.add),
                         r=[r_st5[i]], w=[r_st5[i]])
                    P.op(ACT, lambda e, sc=sc: e.activation(out=sc, in_=sc, func=AF.Ln, scale=1.0 / D, bias=EPS),
                         r=[r_st5[i]], w=[r_st5[i]])
                    P.op(ACT, lambda e, sc=sc: e.activation(out=sc, in_=sc, func=AF.Exp, scale=-0.5), r=[r_st5[i]], w=[r_st5[i]])
                    for c4 in (3, 0, 2, 1):
                        cs = slice(c4 * 512, (c4 + 1) * 512)
                        if c4 in (3, 2):
                            P.op(DVE, lambda e, b=b, sc=sc, cs=cs: e.tensor_scalar(out=xo5[b][:, cs], in0=xo5[b][:, cs], scalar1=sc,
                                                                                   scalar2=None, op0=ALU.mult),
                                 r=[r_xo5p[c4], r_st5[i]], w=[r_xo5p[c4]])
                        else:
                            P.op(ACT, lambda e, b=b, sc=sc, cs=cs: e.activation(out=xo5[b][:, cs], in_=xo5[b][:, cs], func=AF.Copy,
                                                                                scale=sc),
                                 r=[r_xo5p[c4], r_st5[i]], w=[r_xo5p[c4]])
                        stores.append(P.dma(SP, y_d[i * 128:(i + 1) * 128, cs], xo5[b][:, cs], r=[r_xo5p[c4]]))
                    continue
                P.op(ACT, lambda e, b=b, sc=sc: e.activation(out=xt5[b], in_=xo5[b], func=AF.Square, accum_out=sc),
                     r=[r_xo[b]], w=[r_x5[b], r_st5[i]])
                P.op(ACT, lambda e, sc=sc: e.activation(out=sc, in_=sc, func=AF.Ln, scale=1.0 / D, bias=EPS),
                     r=[r_st5[i]], w=[r_st5[i]])
                P.op(ACT, lambda e, sc=sc: e.activation(out=sc, in_=sc, func=AF.Exp, scale=-0.5), r=[r_st5[i]], w=[r_st5[i]])
                if i + 2 < NT:
                    x5_load(i + 2)
                P.op(DVE, lambda e, b=b, sc=sc: e.scalar_tensor_tensor(out=xo5[b], in0=xo5[b], scalar=sc, in1=fgv,
                                                                       op0=ALU.mult, op1=ALU.mult),
                     r=[r_xo[b], r_st5[i], r_fg], w=[r_xo[b]])
                stores.append(P.dma(SP, y_d[i * 128:(i + 1) * 128, :], xo5[b], r=[r_xo[b]]))

        body()
        if limit < 5:
            stores.append(P.dma(SP, y_d[0:128, :], regA[:, 0:2048], r=ph0A + ph1A + cvA + p2all))
        if debug:
            if limit == 0:
                dump("hT", hT[:], [128, NDC, TOK], BF16, r_hT)
                dump("hTh", hTh[:], [128, NDC, 2], BF16, [r_hTh])
            if limit == 1:
                dump("wdec", wdec, [128, NT, 512], F32, r_wdec)
                dump("dec", dec[:], [128, 64], F32, [r_dec])
                dump("dtot", dtot[:], [128, 8], F32, [r_dec])
                dump("kdec", kdec, [128, NT, 512], BF16, r_kdec)
                dump("v", vsb, [128, NT, 1024], BF16, r_v)
                dump("qT", qT, [128, 4, 1024], BF16, r_qT)
            if limit in (3, 4):
                dump("S", S[:], [128, 4, 256], F32, r_S)
                dump("rT", rT, [128, 8, 1024], BF16, r_rT)
                dump("yT", yT[:], [128, 16, TOK], BF16, [x for l in r_yT for x in l])
            for (t, ap, res) in dbgs:
                stores.append(P.dma(SP, t, ap, r=res))
        P.emit(final_waits=stores)
    return nc, P


_CACHE = {}


def _consts():
    c = np.zeros((128, 264), np.float32)
    p = np.arange(128)
    same = (p[:, None] // 64) == (p[None, :] // 64)
    c[:, 0:128] = ((p[:, None] > p[None, :]) & same).astype(np.float32)
    c[:, 128] = (p < 64)
    c[:, 129] = (p >= 64)
    c[:, 130:258] = 1.0
    idb = np.zeros((128, 256), np.float32)
    idb[:, 0:128] = np.eye(128)
    idb[:, 128:256] = 1.0
    return c, idb.astype(ml_dtypes.bfloat16)


def kernel(x, norm_g, w_in, conv_w, conv_b, gla_w_up, gla_b_gate, gla_norm_g, w_out, final_g):
    x = np.asarray(x, np.float32)
    if "nc" not in _CACHE:
        _CACHE["nc"] = build()[0]
    nc = _CACHE["nc"]
    w_in0 = np.ascontiguousarray(np.asarray(w_in, np.float32)[0])
    w_out0 = np.ascontiguousarray(np.asarray(w_out, np.float32)[0])
    gb = np.ascontiguousarray(np.asarray(norm_g, np.float32)[0][None, :])
    fgb = np.ascontiguousarray(np.broadcast_to(np.asarray(final_g, np.float32)[None, :], (128, D)))
    pfm = np.zeros((128, 48), np.float32)
    cwv = np.asarray(conv_w, np.float32)[0]
    for j in range(3):
        pfm[:, j * 8:(j + 1) * 8] = cwv[j].reshape(8, 128).T
    pfm[:, 24:32] = np.asarray(conv_b, np.float32)[0].reshape(8, 128).T
    pfm[:, 32:40] = np.asarray(gla_norm_g, np.float32)[0].reshape(8, 128).T
    wup = np.ascontiguousarray(np.asarray(gla_w_up, np.float32)[0])
    bgt = np.ascontiguousarray(np.asarray(gla_b_gate, np.float32)[0][None, :])
    cst0, idb = _consts()
    in_maps = []
    for c in range(8):
        b, q = c // 4, c % 4
        xc = np.zeros((TOK + 2, D), np.float32)
        xc[2:] = x[b, q * TOK:(q + 1) * TOK]
        if q > 0:
            xc[0:2] = x[b, q * TOK - 2:q * TOK]
        cst = cst0.copy()
        for r in range(3):
            cst[:, 258 + r] = 1.0 if r < q else 0.0
            cst[:, 261 + r] = 0.0 if r < q else 1.0
        in_maps.append({"x": xc, "w_in": w_in0, "w_out": w_out0, "norm_g1": gb, "final_gb": fgb, "pfm": pfm,
                        "w_up": wup, "b_gate": bgt, "cst": cst, "identb": idb})
    res = run_bass_kernel_spmd(nc, in_maps, core_ids=list(range(8)))
    out = np.zeros((2, 4096, D), np.float32)
    for c in range(8):
        b, q = c // 4, c % 4
        out[b, q * TOK:(q + 1) * TOK] = res.results[c]["y"]
    return out
```

```python
import numpy as np
import ml_dtypes
from contextlib import ExitStack
import concourse.bass as bass
import concourse.mybir as mybir
from concourse.bass_utils import run_bass_kernel_spmd

F32 = mybir.dt.float32
BF16 = mybir.dt.bfloat16
F32R = mybir.dt.float32r
ALU = mybir.AluOpType
AF = mybir.ActivationFunctionType

D = 2048
TOK = 1024
NT = 8
NDC = 16
INC = 7184
EPS = 1e-6
C_H, C_B, C_C, C_Z, C_Q, C_K, C_V, C_R, C_GD = 0, 1024, 2048, 3072, 4096, 4608, 5120, 6144, 7168
ENGS = ("pe", "act", "dve", "pool", "sp")


class Res:
    __slots__ = ("name", "writers", "readers", "overlaps")

    def __init__(self, name):
        self.name = name
        self.writers = []
        self.readers = []
        self.overlaps = []


def overlap(a_list, b_list):
    for a in a_list:
        for b in b_list:
            a.overlaps.append(b)
            b.overlaps.append(a)


class Op:
    __slots__ = ("eng", "fn", "deps", "dma", "idx", "sig", "nwait", "name")

    def __init__(self, eng, fn, deps, dma, idx, name):
        self.eng, self.fn, self.deps, self.dma, self.idx, self.name = eng, fn, deps, dma, idx, name
        self.sig = None
        self.nwait = 0


class Prog:
    def __init__(self, nc, n_dma_sems=8):
        self.nc = nc
        self.ops = []
        self.n_dma_sems = n_dma_sems

    def op(self, eng, fn, r=(), w=(), deps=(), dma=False, name=None, join=False):
        dl = {}
        for d in deps:
            if d is not None:
                dl[d.idx] = d
        for res in r:
            for x in [res] + res.overlaps:
                for wr in x.writers:
                    dl[wr.idx] = wr
        for res in w:
            for x in [res] + res.overlaps:
                if not (join and x is res):
                    for wr in x.writers:
                        dl[wr.idx] = wr
                for rd in x.readers:
                    dl[rd.idx] = rd
        dlist = [d for d in dl.values() if not (eng == "pe" and d.eng == "pe")]
        o = Op(eng, fn, dlist, dma, len(self.ops), name)
        for d in dlist:
            d.nwait += 1
        for res in r:
            res.readers.append(o)
        for res in w:
            if join:
                res.writers.append(o)
            else:
                res.writers = [o]
            res.readers = []
        self.ops.append(o)
        return o

    def dma(self, eng, out, in_, r=(), w=(), deps=(), name=None, join=False):
        return self.op(eng, lambda e: e.dma_start(out=out, in_=in_), r, w, deps, dma=True, name=name, join=join)

    def emit(self, final_waits):
        nc = self.nc
        with ExitStack() as st:
            csem = {e: st.enter_context(nc.semaphore("c_" + e)) for e in ENGS}
            dsem = {e: [st.enter_context(nc.semaphore("d_%s%d" % (e, i)))
                        for i in range(self.n_dma_sems)] for e in ("sp", "pool", "act")}
            ccount = {e: 0 for e in ENGS}
            drr = {e: 0 for e in dsem}
            duse = {e: [0] * self.n_dma_sems for e in dsem}
            dprev = {}
            self.op("sp", None, deps=list(final_waits), name="final")
            for o in self.ops:
                if o.dma:
                    k = drr[o.eng]
                    drr[o.eng] = (k + 1) % self.n_dma_sems
                    s = dsem[o.eng][k]
                    if duse[o.eng][k] > 0:
                        dprev[o.idx] = (s, 16 * duse[o.eng][k])
                    duse[o.eng][k] += 1
                    o.sig = (s, 16 * duse[o.eng][k], 16)
                elif o.nwait > 0:
                    ccount[o.eng] += 1
                    o.sig = (csem[o.eng], ccount[o.eng], 1)
            per = {e: [o for o in self.ops if o.eng == e] for e in ENGS}
            self.stats = {e: len(per[e]) for e in ENGS}
            self.stats["sem_max"] = dict(ccount)

            def run(engname, eng):
                seen = {}
                for o in per[engname]:
                    waits = []
                    if o.idx in dprev:
                        waits.append(dprev[o.idx])
                    for d in o.deps:
                        assert d.sig is not None, (o.name, d.name)
                        waits.append((d.sig[0], d.sig[1]))
                    for (s, v) in waits:
                        key = id(s)
                        if seen.get(key, 0) >= v:
                            continue
                        seen[key] = v
                        eng.wait_ge(s, v)
                    if o.fn is None:
                        continue
                    ins = o.fn(eng)
                    if o.sig is not None:
                        assert ins is not None, o.name
                        ins.then_inc(o.sig[0], o.sig[2])

            with nc.Block() as block:
                @block.tensor
                def _(e):
                    run("pe", e)

                @block.scalar
                def _(e):
                    run("act", e)

                @block.vector
                def _(e):
                    run("dve", e)

                @block.gpsimd
                def _(e):
                    run("pool", e)

                @block.sync
                def _(e):
                    run("sp", e)


def build(limit=5, debug=False):
    nc = bass.Bass("TRN2", target_bir_lowering=False)
    dram_in = lambda name, shape, dt=F32: nc.dram_tensor(name, shape, dt, kind="ExternalInput").ap()
    x_d = dram_in("x", [TOK + 2, D])
    win_d = dram_in("w_in", [D, INC])
    wout_d = dram_in("w_out", [D, D])
    gb_d = dram_in("norm_g1", [1, D])
    fg_d = dram_in("final_gb", [128, D])
    pfm_d = dram_in("pfm", [128, 48])
    wup_d = dram_in("w_up", [16, 512])
    bg_d = dram_in("b_gate", [1, 512])
    cst_d = dram_in("cst", [128, 128 + 2 + 128 + 6])
    idb_d = dram_in("identb", [128, 256], BF16)
    idf_d = dram_in("identf", [128, 128])
    y_d = nc.dram_tensor("y", [TOK, D], F32, kind="ExternalOutput").ap()
    cc_in = nc.dram_tensor("cc_in", [128, 1028], F32)
    cc_out = nc.dram_tensor("cc_out", [4 * 128, 1028], F32)

    P = Prog(nc)
    st = ExitStack()
    with st:
        def sb(name, shape, dt):
            return st.enter_context(nc.sbuf_tensor("s_" + name, shape, dt))

        cst = sb("cst", [128, 264], F32)
        idb = sb("idb", [128, 256], BF16)
        idf = sb("idf", [128, 128], F32)
        gdtok = sb("gdtok", [128, NT, 16], F32)
        pfm = sb("pfm", [128, 48], F32)
        wupr = sb("wupr", [33, 512], F32R)
        maskUr = sb("maskUr", [128, 128], F32R)
        stat = sb("stat", [128, 64], F32)
        dec = sb("dec", [128, 64], F32)
        bpre = sb("bpre", [128, 64], F32)
        dtot = sb("dtot", [128, 8], F32)
        hal = sb("hal", [128, 4], F32)
        regA = sb("regA", [128, 8192], F32)
        hT = sb("hT", [128, NDC, TOK], BF16)
        hTh = sb("hTh", [128, NDC, 2], BF16)
        NSL = 3
        wsl = [sb("wsl%d" % i, [128, NDC, 272], BF16) for i in range(NSL)]
        big = sb("big", [128, 16384], F32)
        yT = sb("yT", [128, 16, TOK], BF16)
        S = sb("S", [128, 4, 256], F32)
        Sbf = sb("Sbf", [128, 2, 4, 256], BF16)

        maskU = cst[:, 0:128]
        ind2 = cst[:, 128:130]
        ones_f = cst[:, 130:258]
        msk = cst[:, 258:261]
        omsk = cst[:, 261:264]
        ident = idb[:, 0:128]
        ones_b = idb[:, 128:256]
        cw = lambda j, g: pfm[:, j * 8 + g: j * 8 + g + 1]
        cbias = lambda g: pfm[:, 24 + g: 25 + g]
        gn = lambda hj: pfm[:, 32 + hj: 33 + hj]

        bigb = big[:].bitcast(BF16)
        kdec = bigb[:, 0:4096].rearrange("p (i c) -> p i c", i=NT)
        vsb = bigb[:, 4096:12288].rearrange("p (i c) -> p i c", i=NT)
        qT = bigb[:, 12288:16384].rearrange("p (h t) -> p h t", h=4)
        rT = bigb[:, 16384:24576].rearrange("p (g t) -> p g t", g=8)
        gdT = sb("gdTr", [33, 1024], F32R)
        lr = sb("lr", [128, 2, 512], F32R)
        stg = regA[:, 7168:8192]
        stgB = regA[:, 6144:6656]
        woutsb = bigb.rearrange("p (m c) -> p m c", m=16)
        Ab = regA[:].bitcast(BF16)
        kdecO = Ab[:, 12288:16384].rearrange("p (i c) -> p i c", i=NT)
        NXR = 4
        yTf = yT[:].rearrange("p m t -> p (m t)").bitcast(F32)
        xt0 = [yTf[:, k * 2048:(k + 1) * 2048] for k in range(NXR)]
        xs0 = [Ab[:, 0:2048], Ab[:, 2048:4096]]
        junk0 = Ab[:, 4096:6144]
        gbsb = regA[:, 3072:5120]
        g1 = big[:, 0:2048]
        e_sb = [regA[:, 5120:5632], regA[:, 5632:6144]]
        l_sb = [lr[:, 0, :], lr[:, 1, :]]
        wdec = regA[:, 0:4096].rearrange("p (i c) -> p i c", i=NT)
        gat = regA[:, 0:3 * 1028].rearrange("p (r c) -> p r c", r=3)
        Csb = regA[:, 0:1024]
        usb = regA[:, 1024:2050]
        ysb = regA[:, 2052:3076]
        szb = Ab[:, 6160:7184]
        def p2tmp(k):
            b = 4096 + k * 1024
            sq = Ab[:, 2 * b:2 * b + 512].rearrange("p (a t) -> p a t", a=8)
            rs = regA[:, b + 256:b + 512].rearrange("p (h t) -> p h t", h=4)
            y1 = regA[:, b + 512:b + 1024].rearrange("p (a t) -> p a t", a=8)
            return sq, rs, y1
        xt5 = [regA[:, 0:2048], regA[:, 2048:4096]]
        xo5 = [regA[:, 4096:6144], regA[:, 6144:8192]]

        SP, ACT, DVE, PE, POOL = "sp", "act", "dve", "pe", "pool"
        R = lambda n: Res(n)
        r_cst = R("cst")
        r_xt = [R("xt%d" % k) for k in range(NXR)]
        r_xs = [R("xs0"), R("xs1")]
        r_junk = R("junk")
        r_gb = R("gb")
        r_g1 = R("g1")
        r_st0 = [R("st0_%d" % i) for i in range(9)]
        r_st5 = [R("st5_%d" % i) for i in range(NT)]
        r_dm = R("dm")
        r_dummy = R("dummy")
        r_hT = [R("hT%d" % i) for i in range(NT)]
        r_hTh = R("hTh")
        r_w = [R("wsl%d" % i) for i in range(NSL)]
        r_kdec = [R("kdec%d" % i) for i in range(NT)]
        r_kz = R("kdec_zero")
        r_kdec2 = [[R("kdec%d_%d" % (i, b)) for b in range(2)] for i in range(NT)]
        r_v = [R("v%d" % i) for i in range(NT)]
        r_qT = [R("qT%d" % h) for h in range(4)]
        r_gdT = R("gdT")
        r_gdtok = R("gdtok")
        r_stg = R("stg")
        r_stgB = R("stgB")
        r_wupr = R("wupr")
        r_rT = [R("rT%d" % g) for g in range(8)]
        r_e = [R("e0"), R("e1")]
        r_l = [R("l0"), R("l1")]
        r_wdec = [R("wdec%d" % i) for i in range(NT)]
        r_dec = R("dec")
        r_gat = R("gat")
        r_S = [R("S%d" % h) for h in range(4)]
        r_Sbf = [[R("Sbf%d_%d" % (b, h)) for h in range(4)] for b in range(2)]
        r_yT = [[R("yT%d_%d" % (m, i)) for i in range(NT)] for m in range(16)]
        r_cv = [R("Csb"), R("usb"), R("ysb"), R("szb"), R("hal")]
        r_p2 = [[R("p2_%d_%d" % (k, j)) for j in range(3)] for k in range(2)]
        r_x5 = [R("x5_0"), R("x5_1")]
        r_xo = [R("xo0"), R("xo1")]
        r_xo5p = [R("xo5p%d" % c) for c in range(4)]
        r_wout = [R("wout%d" % c) for c in range(8)]
        r_fg = R("fg")
        r_cc = R("cc")
        p2all = r_p2[0] + r_p2[1]
        ph0A = r_xs + [r_junk, r_gb]
        ph1A = r_e + r_l + r_wdec + [r_gat]
        cvA = r_cv[0:4]
        ph5A = r_x5 + r_xo
        overlap(ph1A, ph0A)
        overlap([r_gat], r_wdec)
        overlap(cvA, ph0A + ph1A)
        overlap(p2all, ph0A + ph1A)
        overlap(ph5A, ph0A + ph1A + cvA + p2all)
        overlap(r_xt, [x for l in r_yT for x in l])
        overlap(r_wout, r_kdec + r_v + r_qT + [r_gdT] + r_rT)
        overlap([r_fg], r_w)
        overlap([r_stg, r_stgB], ph5A)
        overlap(r_xo5p, [r_xo[1]])
        overlap([r_g1], r_kdec + r_wout)
        allk2 = [x for l in r_kdec2 for x in l]
        overlap(allk2 + r_kdec + [r_kz], [r_stg, r_stgB] + ph5A)
        overlap(allk2, r_wout + [r_g1])

        pbank = [st.enter_context(nc.psum_tensor("pb%d" % i, [128, 512], F32)) for i in range(8)]
        r_pb = [R("pb%d" % i) for i in range(8)]
        r_mbh = R("mb_halo")
        r_mbs = R("mb_ss")

        stores = []
        dbgs = []

        def dump(name, ap, shape, dt, res):
            t = nc.dram_tensor("dbg_" + name, shape, dt, kind="ExternalOutput").ap()
            dbgs.append((t, ap, res))

        def body():
            P.op(DVE, lambda e: e.memset(stat[:, 56:64], 1.0), w=[r_dummy])
            def load_consts():
                P.dma(SP, cst[:], cst_d, w=[r_cst])
                P.dma(SP, idb[:], idb_d, w=[r_cst], join=True)
                P.dma(SP, idf[:], idf_d, w=[r_cst], join=True)
                P.dma(SP, pfm[:], pfm_d, w=[r_cst], join=True)

            wcols = lambda c0, n: win_d[:, c0:c0 + n]
            wconv = win_d[:, 0:4096].rearrange("(dc p) (s t g j) -> p dc s t g j", p=128, s=2, t=2, g=8, j=128)
            blk256 = lambda c0: [(wcols(c0, 256).rearrange("(dc p) c -> p dc c", p=128), 0, 256)]
            wplan = [blk256(C_V) + [(wcols(C_GD, 16).rearrange("(dc p) c -> p dc c", p=128), 256, 16)]]
            wplan += [blk256(C_V + b * 256) for b in range(1, 4)]
            wplan += [blk256(C_K + b * 256) for b in range(2)]
            wplan += [blk256(C_Q + b * 256) for b in range(2)]
            wplan += [blk256(C_R + b * 256) for b in range(4)]
            for g in range(8):
                for t in range(2):
                    wplan.append([(wconv[:, :, sg, t, g, :], sg * 128, 128) for sg in range(2)])
            wst = {"loaded": 0, "used": 0}

            def w_emit(deps=()):
                n = wst["loaded"]
                if n >= len(wplan):
                    return
                wst["loaded"] += 1
                s = n % NSL
                for k, (src, c0, nn) in enumerate(wplan[n]):
                    P.dma(POOL, wsl[s][:, :, c0:c0 + nn], src, w=[r_w[s]], join=(k > 0), deps=deps)

            def wload():
                n = wst["used"]
                wst["used"] += 1
                assert n < wst["loaded"]
                return n % NSL

            def wdone(k=1):
                for _ in range(k):
                    w_emit()

            def x_load(i):
                if i < NT:
                    return P.dma(SP, xt0[i % NXR], x_d[2 + i * 128: 2 + (i + 1) * 128, :], w=[r_xt[i % NXR]])
                return P.dma(SP, xt0[i % NXR][0:2, :], x_d[0:2, :], w=[r_xt[i % NXR]])

            def st1(i, npart):
                xk = i % NXR
                sc = stat[0:npart, i:i + 1]
                P.op(ACT, lambda e: e.activation(out=junk0[0:npart, :], in_=xt0[xk][0:npart, :], func=AF.Square,
                                                 accum_out=sc), r=[r_xt[xk]], w=[r_junk, r_st0[i]])
                P.op(ACT, lambda e: e.activation(out=sc, in_=sc, func=AF.Ln, scale=1.0 / D, bias=EPS),
                     r=[r_st0[i]], w=[r_st0[i]])
                P.op(ACT, lambda e: e.activation(out=sc, in_=sc, func=AF.Exp, scale=-0.5), r=[r_st0[i]], w=[r_st0[i]])

            def st2(i, npart):
                b = i % 2
                xk = i % NXR
                sc = stat[0:npart, i:i + 1]
                P.op(DVE, lambda e: e.scalar_tensor_tensor(out=xs0[b][0:npart, :], in0=xt0[xk][0:npart, :], scalar=sc,
                                                           in1=gbsb[0:npart, :], op0=ALU.mult, op1=ALU.mult),
                     r=[r_xt[xk], r_st0[i], r_gb], w=[r_xs[b]])

            def st3(i):
                if i == NT:
                    pvh = pbank[0][:].bitcast(BF16)

                    def trh(e):
                        ins = None
                        for dc in range(16):
                            ins = e.transpose(out=pvh[:, dc * 2:dc * 2 + 2], in_=xs0[0][0:2, dc * 128:(dc + 1) * 128],
                                              identity=ident[0:2, 0:2])
                        return ins
                    P.op(PE, trh, r=[r_xs[0], r_cst], w=[r_pb[0]])
                    P.op(DVE, lambda e: e.tensor_copy(out=hTh[:], in_=pvh[:, 0:32].rearrange("p (d t) -> p d t", d=16)),
                         r=[r_pb[0]], w=[r_hTh])
                    return
                for half in range(2):
                    bk = (2 * i + half) % 4
                    pv = pbank[bk][:].bitcast(BF16)

                    def tr(e, half=half, pv=pv):
                        ins = None
                        for j in range(8):
                            dc = half * 8 + j
                            ins = e.transpose(out=pv[:, j * 128:(j + 1) * 128],
                                              in_=xs0[i % 2][:, dc * 128:(dc + 1) * 128], identity=ident)
                        return ins
                    P.op(PE, tr, r=[r_xs[i % 2], r_cst], w=[r_pb[bk]])
                    dst = hT[:, half * 8:(half + 1) * 8, i * 128:(i + 1) * 128]
                    srcv = pv.rearrange("p (j t) -> p j t", j=8)
                    if half == 0:
                        P.op(ACT, lambda e, dst=dst, srcv=srcv: e.activation(out=dst, in_=srcv, func=AF.Copy),
                             r=[r_pb[bk]], w=[r_hT[i]])
                    else:
                        P.op(DVE, lambda e, dst=dst, srcv=srcv: e.tensor_copy(out=dst, in_=srcv),
                             r=[r_pb[bk]], w=[r_hT[i]], join=True)

            ring = {"n": 0, "k": 6}

            def nb():
                b = ring["n"] % ring["k"]
                ring["n"] += 1
                return b

            def featmajor(s, col0, halo_bank=None, halo_col=0, mid=None):
                ba, bb = nb(), nb()
                first = 4 if (mid is not None and halo_bank is not None) else 0
                last = (first - 1) % NDC

                def mm(e, dcs, halo):
                    ins = None
                    for dc in dcs:
                        lw = wsl[s][:, dc, col0:col0 + 128]
                        ins = e.matmul(pbank[ba][:], lhsT=lw, rhs=hT[:, dc, 0:512], start=(dc == 0), stop=(dc == NDC - 1))
                        ins = e.matmul(pbank[bb][:], lhsT=lw, rhs=hT[:, dc, 512:1024], start=(dc == 0), stop=(dc == NDC - 1))
                        if halo:
                            ins = e.matmul(pbank[halo_bank][:, halo_col:halo_col + 2], lhsT=lw, rhs=hTh[:, dc, :],
                                           start=(dc == first), stop=(dc == last))
                    return ins

                def halo_only(e, dcs):
                    ins = None
                    for dc in dcs:
                        ins = e.matmul(pbank[halo_bank][:, halo_col:halo_col + 2], lhsT=wsl[s][:, dc, col0:col0 + 128],
                                       rhs=hTh[:, dc, :], start=(dc == first), stop=(dc == last))
                    return ins
                hb = halo_bank is not None
                wr = [r_pb[ba], r_pb[bb]]
                rd = [r_w[s], r_hTh] + r_hT
                if mid is None:
                    P.op(PE, lambda e: mm(e, range(NDC), hb), r=rd, w=wr + ([r_pb[halo_bank]] if hb else []))
                else:
                    P.op(PE, lambda e: mm(e, range(0, 4), False), r=rd, w=wr)
                    mid()
                    if hb:
                        P.op(PE, lambda e: (mm(e, range(4, NDC), True), halo_only(e, range(0, 4)))[1], r=rd,
                             w=wr + [r_pb[halo_bank]])
                    else:
                        P.op(PE, lambda e: mm(e, range(4, NDC), False), r=rd, w=wr)
                return ba, bb

            def tok_unit(s, i, evac, bk=None, n=256):
                if bk is None:
                    bk = nb()

                def mm(e):
                    ins = None
                    for dc in range(NDC):
                        ins = e.matmul(pbank[bk][:, 0:n], lhsT=hT[:, dc, i * 128:(i + 1) * 128],
                                       rhs=wsl[s][:, dc, 0:n], start=(dc == 0), stop=(dc == NDC - 1))
                    return ins
                P.op(PE, mm, r=[r_w[s], r_hT[i]], w=[r_pb[bk]])
                evac(i, bk)

            P.dma(SP, g1[0:1, :], gb_d, w=[r_g1])
            load_consts()
            x_load(0)
            P.op(DVE, lambda e: e.memset(stg[0:64, :], 0.0), w=[r_stg])
            P.op(DVE, lambda e: e.memset(stg[32:33, :], 1.0), w=[r_stg])
            P.op(DVE, lambda e: e.memset(stgB[0:64, :], 0.0), w=[r_stgB])
            for c in range(4):
                P.op(PE, lambda e, c=c: e.matmul(pbank[4 + c][:], lhsT=ones_f[0:1, :], rhs=g1[0:1, c * 512:(c + 1) * 512],
                                                 start=True, stop=True), r=[r_cst, r_g1], w=[r_pb[4 + c]])
                P.op(DVE, lambda e, c=c: e.tensor_copy(out=gbsb[:, c * 512:(c + 1) * 512], in_=pbank[4 + c][:]),
                     r=[r_pb[4 + c]], w=[r_gb], join=(c > 0))
            x_load(1)
            x3op = x_load(2)
            x_load(3)
            P.dma(SP, stgB[0:16, :], wup_d, w=[r_stgB])
            P.dma(SP, stgB[32:33, :], bg_d, w=[r_stgB], join=True)
            for _ in range(NSL):
                w_emit(deps=[x3op])

            def make_evac_v(blk, gd=False):
                def evac_v(i, bk):
                    if (i + blk) % 2 == 0:
                        P.op(ACT, lambda e: e.activation(out=vsb[:, i, blk * 256:(blk + 1) * 256],
                                                         in_=pbank[bk][:, 0:256], func=AF.Copy),
                             r=[r_pb[bk]], w=[r_v[i]])
                        if gd:
                            P.op(ACT, lambda e: e.activation(out=gdtok[:, i, :], in_=pbank[bk][:, 256:272], func=AF.Copy),
                                 r=[r_pb[bk]], w=[r_gdtok], join=(i > 0))
                    else:
                        P.op(DVE, lambda e: e.tensor_copy(out=vsb[:, i, blk * 256:(blk + 1) * 256],
                                                          in_=pbank[bk][:, 0:256]), r=[r_pb[bk]], w=[r_v[i]])
                        if gd:
                            P.op(DVE, lambda e: e.tensor_copy(out=gdtok[:, i, :], in_=pbank[bk][:, 256:272]),
                                 r=[r_pb[bk]], w=[r_gdtok], join=(i > 0))
                return evac_v

            s_v0 = wload()
            npt = lambda i: 128 if i < NT else 2
            for t in range(NT + 8):
                if t <= NT:
                    st1(t, npt(t))
                if 0 <= t - 1 <= NT:
                    st2(t - 1, npt(t - 1))
                if 4 <= t + 3 <= NT:
                    x_load(t + 3)
                if 0 <= t - 2 <= NT:
                    st3(t - 2)
                if 0 <= t - 8 < NT:
                    tok_unit(s_v0, t - 8, make_evac_v(0, gd=True), bk=4 + (t - 8) % 4, n=272)
            wdone()
            if limit < 1:
                return

            P.op(ACT, lambda e: e.activation(out=gdT[0:33, :], in_=stg[0:33, :], func=AF.Copy), r=[r_stg], w=[r_gdT])
            P.op(ACT, lambda e: e.activation(out=wupr[0:33, :], in_=stgB[0:33, :], func=AF.Copy), r=[r_stgB], w=[r_wupr])
            P.op(ACT, lambda e: e.activation(out=maskUr[:], in_=maskU, func=AF.Copy), r=[r_cst], w=[r_wupr], join=True)
            P.op(POOL, lambda e: e.memset(kdecO[0:64, :, :], 0.0), r=[r_stg, r_stgB], w=[r_kz] + r_kdec)
            b0, b1 = nb(), nb()

            def gdtr(e):
                ins = None
                for i in range(NT):
                    bk = b0 if i < 4 else b1
                    ins = e.matmul(pbank[bk][0:16, (i % 4) * 128:(i % 4 + 1) * 128], lhsT=gdtok[:, i, :], rhs=idf[:],
                                   start=True, stop=True)
                return ins
            P.op(PE, gdtr, r=[r_gdtok, r_cst], w=[r_pb[b0], r_pb[b1]])
            P.op(ACT, lambda e: e.activation(out=gdT[0:16, 0:512], in_=pbank[b0][0:16, :], func=AF.Copy),
                 r=[r_pb[b0]], w=[r_gdT])
            P.op(ACT, lambda e: e.activation(out=gdT[0:16, 512:1024], in_=pbank[b1][0:16, :], func=AF.Copy),
                 r=[r_pb[b1]], w=[r_gdT])

            def gen_v():
                for blk in range(1, 4):
                    s = wload()
                    for i in range(NT):
                        tok_unit(s, i, make_evac_v(blk))
                        if i == NT - 1:
                            wdone()
                        yield
            gv = gen_v()
            next(gv)
            next(gv)

            for i in range(NT):
                j = i % 2
                bx = nb()
                P.op(PE, lambda e, i=i, bx=bx: e.matmul(pbank[bx][:], lhsT=gdT[0:33, i * 128:(i + 1) * 128],
                                                        rhs=wupr[0:33, :], start=True, stop=True),
                     r=[r_gdT, r_wupr], w=[r_pb[bx]])
                P.op(ACT, lambda e, j=j, bx=bx: e.activation(out=e_sb[j], in_=pbank[bx][:], func=AF.Exp, scale=-1.0),
                     r=[r_pb[bx]], w=[r_e[j]])
                P.op(ACT, lambda e, j=j: e.activation(out=l_sb[j], in_=e_sb[j], func=AF.Ln, bias=1.0),
                     r=[r_e[j]], w=[r_l[j]])
                next(gv)
                next(gv)
                brv, bbe = nb(), nb()
                P.op(PE, lambda e, j=j, brv=brv: e.matmul(pbank[brv][:], lhsT=maskUr[:], rhs=l_sb[j],
                                                          start=True, stop=True),
                     r=[r_l[j], r_wupr], w=[r_pb[brv]])
                P.op(ACT, lambda e, i=i, brv=brv: e.activation(out=wdec[:, i, :], in_=pbank[brv][:], func=AF.Exp,
                                                               scale=-1.0 / 16.0),
                     r=[r_pb[brv]], w=[r_wdec[i]])

                def bem(e, j=j, bbe=bbe):
                    ins = None
                    for h in range(4):
                        ins = e.matmul(pbank[bbe][:, h * 2:h * 2 + 2], lhsT=l_sb[j][:, h * 128:(h + 1) * 128].bitcast(F32), rhs=ind2,
                                       start=True, stop=True)
                    return ins
                P.op(PE, bem, r=[r_l[j], r_cst], w=[r_pb[bbe]])
                P.op(DVE, lambda e, i=i, bbe=bbe: e.tensor_copy(out=bpre[:, i * 8:(i + 1) * 8], in_=pbank[bbe][:, 0:8]),
                     r=[r_pb[bbe]], w=[r_dec])
                if i < 6:
                    next(gv)
            for _ in gv:
                pass
            P.op(ACT, lambda e: e.activation(out=dec[:], in_=bpre[:], func=AF.Exp, scale=-1.0 / 16.0), r=[r_dec], w=[r_dec])
            P.op(DVE, lambda e: e.tensor_reduce(out=dtot[:, 4:8], in_=bpre[:].rearrange("p (i h s) -> p h i s", i=8, h=4),
                                                axis=mybir.AxisListType.XY, op=ALU.add), r=[r_dec], w=[r_dec])
            P.op(ACT, lambda e: e.activation(out=dtot[:, 0:4], in_=dtot[:, 4:8], func=AF.Exp, scale=-1.0 / 16.0),
                 r=[r_dec], w=[r_dec])

            UB1 = [6, 7]

            def p1_chunk(n):
                i, s_ = n // 2, n % 2
                ps0, ps1 = s_ * 64, s_ * 64 + 64
                for h in range(4):
                    ub, uo = UB1[h // 2], (h % 2) * 256
                    P.op(PE, lambda e, i=i, h=h, ub=ub, uo=uo: e.matmul(
                        pbank[ub][:, uo:uo + 256], lhsT=(kdec, kdecO)[s_][:, i, h * 128:(h + 1) * 128],
                        rhs=vsb[:, i, h * 256:(h + 1) * 256], start=True, stop=True),
                        r=[r_kdec2[i][h // 2], r_v[i], r_kz], w=[r_pb[ub]])
                for h in range(4):
                    ub, uo = UB1[h // 2], (h % 2) * 256
                    P.op(DVE, lambda e, i=i, h=h, ub=ub, uo=uo: e.scalar_tensor_tensor(
                        out=S[:, h, :], in0=S[:, h, :], scalar=dec[:, i * 8 + h * 2 + s_: i * 8 + h * 2 + s_ + 1],
                        in1=pbank[ub][:, uo:uo + 256], op0=ALU.mult, op1=ALU.add),
                        r=[r_S[h], r_dec, r_pb[ub]], w=[r_S[h]])

            ring["n"] = 0
            P.op(DVE, lambda e: e.memset(S[:], 0.0), w=r_S)
            s_k = [wload(), wload()]
            for i in range(NT):
                for blk in range(2):
                    def evac_k(i, bk, blk=blk):
                        cs = slice(blk * 256, (blk + 1) * 256)
                        rk = r_kdec2[i][blk]
                        P.op(DVE, lambda e: e.tensor_tensor(out=kdec[:, i, cs], in0=pbank[bk][:, 0:256],
                                                            in1=wdec[:, i, cs], op=ALU.mult),
                             r=[r_pb[bk], r_wdec[i]], w=[rk])
                        P.op(POOL, lambda e: e.tensor_copy(out=kdecO[64:128, i, cs], in_=kdec[64:128, i, cs]),
                             r=[rk, r_kz], w=[rk])
                        P.op(POOL, lambda e: e.memset(kdec[64:128, i, cs], 0.0), w=[rk])
                    tok_unit(s_k[blk], i, evac_k)
                    if i >= 1:
                        p1_chunk(2 * (i - 1) + blk)
                if i == NT - 1:
                    wdone(2)

            def q_group(blk, hh, s):
                h = blk * 2 + hh
                ba, bb = featmajor(s, hh * 128)
                for tb, bk in ((0, ba), (1, bb)):
                    P.op(ACT, lambda e, h=h, tb=tb, bk=bk: e.activation(out=qT[:, h, tb * 512:(tb + 1) * 512],
                                                                        in_=pbank[bk][:], func=AF.Copy,
                                                                        scale=float(128 ** -0.5)),
                         r=[r_pb[bk]], w=[r_qT[h]])
            s_q0 = wload()
            q_group(0, 0, s_q0)
            p1_chunk(14)
            p1_chunk(15)
            if limit >= 2:
                P.dma(SP, cc_in.ap()[:, 0:1024], S[:].rearrange("p h v -> p (h v)"), r=r_S, w=[r_cc])
                P.dma(SP, cc_in.ap()[:, 1024:1028], dtot[:, 0:4], r=[r_dec], w=[r_cc])
            q_group(0, 1, s_q0)
            wdone()
            s_q1 = wload()
            if limit >= 2:
                P.op(POOL, lambda e: e.collective_compute("AllGather", ALU.bypass,
                                                          replica_groups=[[0, 1, 2, 3], [4, 5, 6, 7]],
                                                          ins=[cc_in.ap().opt()], outs=[cc_out.ap().opt()]),
                     r=[r_cc], w=[r_cc])
            q_group(1, 0, s_q1)
            q_group(1, 1, s_q1)
            wdone()
            if limit < 2:
                return

            def combine():
                P.dma(SP, gat, cc_out.ap()[0:384, :].rearrange("(r p) c -> p r c", p=128), r=[r_cc], w=[r_gat])
                P.op(DVE, lambda e: e.memset(S[:], 0.0), w=r_S)
                dm = stat[:, 16:28]
                for r_ in range(3):
                    P.op(DVE, lambda e, r_=r_: e.tensor_scalar(out=dm[:, r_ * 4:(r_ + 1) * 4], in0=gat[:, r_, 1024:1028],
                                                               scalar1=msk[:, r_:r_ + 1], scalar2=omsk[:, r_:r_ + 1],
                                                               op0=ALU.mult, op1=ALU.add), r=[r_gat, r_cst], w=[r_dm])
                    P.op(DVE, lambda e, r_=r_: e.tensor_scalar(out=gat[:, r_, 0:1024], in0=gat[:, r_, 0:1024],
                                                               scalar1=msk[:, r_:r_ + 1], scalar2=None, op0=ALU.mult),
                         r=[r_gat, r_cst], w=[r_gat])
                    for h in range(4):
                        P.op(DVE, lambda e, r_=r_, h=h: e.scalar_tensor_tensor(
                            out=S[:, h, :], in0=S[:, h, :], scalar=dm[:, r_ * 4 + h:r_ * 4 + h + 1],
                            in1=gat[:, r_, h * 256:(h + 1) * 256], op0=ALU.mult, op1=ALU.add),
                            r=[r_gat, r_dm, r_S[h]], w=[r_S[h]])

            for blk in range(4):
                s = wload()
                for hh in range(2):
                    g = blk * 2 + hh
                    ba, bb = featmajor(s, hh * 128)
                    for tb, bk in ((0, ba), (1, bb)):
                        P.op(ACT, lambda e, g=g, tb=tb, bk=bk: e.activation(out=rT[:, g, tb * 512:(tb + 1) * 512],
                                                                            in_=pbank[bk][:], func=AF.Silu),
                             r=[r_pb[bk]], w=[r_rT[g]])
                    P.op(DVE, lambda e, g=g: e.tensor_scalar(out=rT[:, g, :], in0=rT[:, g, :], scalar1=gn(g), scalar2=None,
                                                             op0=ALU.mult), r=[r_rT[g], r_cst], w=[r_rT[g]])
                    if hh == 1:
                        wdone()
                        if blk == 2:
                            combine()

            if debug:
                dump("Sin", S[:], [128, 4, 256], F32, r_S)
                for (t, ap, res) in dbgs:
                    stores.append(P.dma(SP, t, ap, r=res))
                del dbgs[:]
            if limit < 3:
                return

            MB = 4
            UB2 = 5
            OB = [6, 7]

            def p2_A(n, hp):
                i, s_ = n // 2, n % 2
                ps0, ps1 = s_ * 64, s_ * 64 + 64
                bf = n % 2
                for hh in range(2):
                    h = hp * 2 + hh
                    uo = hh * 256
                    P.op(PE, lambda e, i=i, h=h, uo=uo: e.matmul(
                        pbank[UB2][:, uo:uo + 256], lhsT=(kdec, kdecO)[s_][:, i, h * 128:(h + 1) * 128],
                        rhs=vsb[:, i, h * 256:(h + 1) * 256], start=True, stop=True),
                        r=[r_kdec2[i][h // 2], r_v[i], r_kz], w=[r_pb[UB2]])
                for hh in range(2):
                    h = hp * 2 + hh
                    uo = hh * 256
                    P.op(DVE, lambda e, i=i, h=h, uo=uo: e.scalar_tensor_tensor(
                        out=S[:, h, :], in0=S[:, h, :], scalar=dec[:, i * 8 + h * 2 + s_: i * 8 + h * 2 + s_ + 1],
                        in1=pbank[UB2][:, uo:uo + 256], op0=ALU.mult, op1=ALU.add),
                        r=[r_S[h], r_dec, r_pb[UB2]], w=[r_S[h]])
                    P.op(ACT, lambda e, h=h, bf=bf: e.activation(out=Sbf[:, bf, h, :], in_=S[:, h, :], func=AF.Copy),
                         r=[r_S[h]], w=[r_Sbf[bf][h]])

            def p2_B(n):
                bf = n % 2
                ob = OB[n % 2]

                def om(e):
                    ins = None
                    for h in range(4):
                        for j in range(2):
                            hj = h * 2 + j
                            ins = e.matmul(pbank[ob][:, hj * 64:(hj + 1) * 64], lhsT=Sbf[:, bf, h, j * 128:(j + 1) * 128],
                                           rhs=qT[:, h, n * 64:(n + 1) * 64], start=True, stop=True)
                    return ins
                P.op(PE, om, r=r_Sbf[bf] + r_qT, w=[r_pb[ob]])
                sq, rs, y1 = p2tmp(n % 2)
                rp = r_p2[n % 2]
                P.op(ACT, lambda e: e.activation(out=sq, in_=pbank[ob][:].rearrange("p (a t) -> p a t", a=8),
                                                 func=AF.Square), r=[r_pb[ob]], w=[rp[0]])

            def p2_C(n):
                ob = OB[n % 2]
                sq, rs, y1 = p2tmp(n % 2)
                rp = r_p2[n % 2]

                def ssm(e):
                    ins = None
                    for h in range(4):
                        for j in range(2):
                            ins = e.matmul(pbank[MB][:, 256 + h * 64:256 + (h + 1) * 64], lhsT=ones_b, rhs=sq[:, h * 2 + j, :],
                                           start=(j == 0), stop=(j == 1))
                    return ins
                P.op(PE, ssm, r=[rp[0], r_cst], w=[r_pb[MB]])
                P.op(ACT, lambda e: e.activation(out=rs, in_=pbank[MB][:, 256:512].rearrange("p (h t) -> p h t", h=4),
                                                 func=AF.Ln, scale=1.0 / 256.0, bias=EPS), r=[r_pb[MB]], w=[rp[1]])
                P.op(ACT, lambda e: e.activation(out=rs, in_=rs, func=AF.Exp, scale=-0.5), r=[rp[1]], w=[rp[1]])
                P.op(DVE, lambda e: e.tensor_tensor(
                    out=y1.rearrange("p (h j) t -> p h j t", h=4),
                    in0=pbank[ob][:].rearrange("p (h j t) -> p h j t", h=4, j=2),
                    in1=rs.unsqueeze(2).to_broadcast([128, 4, 2, 64]), op=ALU.mult),
                    r=[r_pb[ob], rp[1]], w=[rp[2]])
                i = n // 2
                P.op(DVE, lambda e: e.tensor_tensor(out=yT[:, 8:16, n * 64:(n + 1) * 64], in0=y1,
                                                    in1=rT[:, :, n * 64:(n + 1) * 64], op=ALU.mult),
                     r=[rp[2]] + r_rT, w=[r_yT[m][i] for m in range(8, 16)])

            def p2_pre(k):
                if 0 <= k - 2 < 16:
                    p2_B(k - 2)

            def p2_mid(k):
                if 0 <= k < 16:
                    p2_A(k, 0)

            def p2_post(k):
                if 0 <= k < 16:
                    p2_A(k, 1)
                if 0 <= k - 2 < 16:
                    p2_C(k - 2)

            ring["n"] = 0
            ring["k"] = 4
            unit = {"k": 0}

            P2LAG = 1

            def side():
                p2_pre(unit["k"] - P2LAG)

            def mid():
                p2_mid(unit["k"] - P2LAG)

            def post():
                p2_post(unit["k"] - P2LAG)
                unit["k"] += 1

            WSCHED = {5: (2, 3, 4), 6: (5, 6, 7)}
            WSCHED_END = {4: (0, 1)}
            for g in range(8):
                s1 = wload()
                hc = g * 4
                side()
                ca, cbk = featmajor(s1, 128, MB, hc, mid=mid)
                P.op(ACT, lambda e, hc=hc: e.activation(out=hal[:, 0:2], in_=pbank[MB][:, hc:hc + 2], func=AF.Copy),
                     r=[r_pb[MB]], w=[r_cv[4]])
                post()
                P.op(ACT, lambda e, ca=ca: e.activation(out=Csb[:, 0:512], in_=pbank[ca][:], func=AF.Copy),
                     r=[r_pb[ca]], w=[r_cv[0]])
                P.op(ACT, lambda e, cbk=cbk: e.activation(out=Csb[:, 512:1024], in_=pbank[cbk][:], func=AF.Copy),
                     r=[r_pb[cbk]], w=[r_cv[0]])
                side()
                ha, hb_ = featmajor(s1, 0, MB, hc + 2, mid=mid)
                P.op(ACT, lambda e, hc=hc: e.activation(out=hal[:, 2:4], in_=pbank[MB][:, hc + 2:hc + 4], func=AF.Copy),
                     r=[r_pb[MB]], w=[r_cv[4]], join=True)
                post()
                P.op(DVE, lambda e: e.tensor_tensor(out=usb[:, 0:2], in0=hal[:, 2:4], in1=hal[:, 0:2], op=ALU.mult),
                     r=[r_cv[4]], w=[r_cv[1]])
                P.op(DVE, lambda e, ha=ha: e.tensor_tensor(out=usb[:, 2:514], in0=pbank[ha][:], in1=Csb[:, 0:512],
                                                           op=ALU.mult), r=[r_pb[ha], r_cv[0]], w=[r_cv[1]])
                P.op(DVE, lambda e, hb_=hb_: e.tensor_tensor(out=usb[:, 514:1026], in0=pbank[hb_][:], in1=Csb[:, 512:1024],
                                                             op=ALU.mult), r=[r_pb[hb_], r_cv[0]], w=[r_cv[1]])
                P.op(DVE, lambda e, g=g: e.tensor_scalar(out=ysb, in0=usb[:, 2:1026], scalar1=cw(2, g),
                                                         scalar2=cbias(g), op0=ALU.mult, op1=ALU.add),
                     r=[r_cv[1], r_cst], w=[r_cv[2]])
                P.op(DVE, lambda e, g=g: e.scalar_tensor_tensor(out=ysb, in0=usb[:, 1:1025], scalar=cw(1, g),
                                                                in1=ysb, op0=ALU.mult, op1=ALU.add),
                     r=[r_cv[1], r_cv[2], r_cst], w=[r_cv[2]])
                P.op(DVE, lambda e, g=g: e.scalar_tensor_tensor(out=ysb, in0=usb[:, 0:1024], scalar=cw(0, g),
                                                                in1=ysb, op0=ALU.mult, op1=ALU.add),
                     r=[r_cv[1], r_cv[2], r_cst], w=[r_cv[2]])
                wdone()
                if limit >= 5 and g in WSCHED:
                    for cb in WSCHED[g]:
                        P.dma(POOL, woutsb[:, :, cb * 256:(cb + 1) * 256],
                              wout_d[:, cb * 256:(cb + 1) * 256].rearrange("(m p) c -> p m c", p=128), w=[r_wout[cb]])
                s2 = wload()
                side()
                Ba, Bb = featmajor(s2, 0, mid=mid)
                post()
                for tb, bk in ((0, Ba), (1, Bb)):
                    P.op(DVE, lambda e, tb=tb, bk=bk: e.tensor_tensor(out=ysb[:, tb * 512:(tb + 1) * 512], in0=pbank[bk][:],
                                                                      in1=ysb[:, tb * 512:(tb + 1) * 512], op=ALU.mult),
                         r=[r_pb[bk], r_cv[2]], w=[r_cv[2]])
                side()
                za, zb = featmajor(s2, 128, mid=mid)
                post()
                for tb, bk in ((0, za), (1, zb)):
                    P.op(ACT, lambda e, tb=tb, bk=bk: e.activation(out=szb[:, tb * 512:(tb + 1) * 512], in_=pbank[bk][:],
                                                                   func=AF.Silu), r=[r_pb[bk]], w=[r_cv[3]])
                if unit["k"] < 19 + P2LAG:
                    P.op(ACT, lambda e: e.activation(out=stat[:, 61:62], in_=stat[:, 60:61], func=AF.Ln, bias=1.0),
                         w=[r_dummy])
                P.op(DVE, lambda e, g=g: e.tensor_tensor(out=yT[:, g, :], in0=ysb, in1=szb, op=ALU.mult),
                     r=[r_cv[2], r_cv[3]], w=[r_yT[g][i] for i in range(NT)])
                wdone()
                if limit >= 5 and g in WSCHED_END:
                    for cb in WSCHED_END[g]:
                        P.dma(POOL, woutsb[:, :, cb * 256:(cb + 1) * 256],
                              wout_d[:, cb * 256:(cb + 1) * 256].rearrange("(m p) c -> p m c", p=128), w=[r_wout[cb]])
            while unit["k"] < 18 + P2LAG:
                side()
                mid()
                post()
            if limit < 5:
                return

            fgv = wsl[0][:].rearrange("p a b -> p (a b)").bitcast(F32)[:, 0:2048]
            P.dma(SP, fgv, fg_d, w=[r_fg])

            def x5_load(i):
                P.dma(SP, xt5[i % 2], x_d[2 + i * 128: 2 + (i + 1) * 128, :], w=[r_x5[i % 2]])
            x5_load(0)
            x5_load(1)
            for i in range(NT):
                b = i % 2
                base = 0 if i % 2 == 0 else 4
                for c4 in range(4):
                    bk = base + c4

                    def om(e, i=i, c4=c4, bk=bk, ms=()):
                        ins = None
                        for m in ms:
                            ins = e.matmul(pbank[bk][:], lhsT=yT[:, m, i * 128:(i + 1) * 128],
                                           rhs=woutsb[:, m, c4 * 512:(c4 + 1) * 512], start=(m == 8), stop=(m == 7))
                        return ins
                    ms1, ms2 = list(range(8, 16)), list(range(0, 8))
                    P.op(PE, lambda e, om=om, ms1=ms1: om(e, ms=ms1),
                         r=[r_yT[m][i] for m in ms1] + [r_wout[2 * c4], r_wout[2 * c4 + 1]], w=[r_pb[bk]])
                    P.op(PE, lambda e, om=om, ms2=ms2: om(e, ms=ms2),
                         r=[r_yT[m][i] for m in ms2] + [r_wout[2 * c4], r_wout[2 * c4 + 1]], w=[r_pb[bk]])
                    P.op(DVE, lambda e, b=b, c4=c4, bk=bk: e.tensor_tensor(out=xo5[b][:, c4 * 512:(c4 + 1) * 512],
                                                                           in0=pbank[bk][:],
                                                                           in1=xt5[b][:, c4 * 512:(c4 + 1) * 512], op=ALU.add),
                         r=[r_pb[bk], r_x5[b]], w=[r_xo[b]], join=(c4 > 0))
                    if i == NT - 1:
                        cs = slice(c4 * 512, (c4 + 1) * 512)
                        P.op(ACT, lambda e, b=b, c4=c4, cs=cs: e.activation(out=xt5[b][:, cs], in_=xo5[b][:, cs], func=AF.Square,
                                                                            accum_out=stat[:, 44 + c4:45 + c4]),
                             r=[r_xo[b]], w=[r_x5[b], r_st5[i]], join=(c4 > 0))
                        P.op(DVE, lambda e, b=b, cs=cs: e.tensor_tensor(out=xo5[b][:, cs], in0=xo5[b][:, cs], in1=fgv[:, cs],
                                                                        op=ALU.mult),
                             r=[r_xo[b], r_fg], w=[r_xo5p[c4]])
                sc = stat[:, 32 + i:33 + i]
                if i == NT - 1:
                    P.op(DVE, lambda e, sc=sc: e.tensor_reduce(out=sc, in_=stat[:, 44:48], axis=mybir.AxisListType.X, op=ALU.add),
                         r=[r_st5[i]], w=[r_st5[i]])
                    P.op(ACT, lambda e, sc=sc: e.activation(out=sc, in_=sc, func=AF.Ln, scale=1.0 / D, bias=EPS),
                         r=[r_st5[i]], w=[r_st5[i]])
                    P.op(ACT, lambda e, sc=sc: e.activation(out=sc, in_=sc, func=AF.Exp, scale=-0.5), r=[r_st5[i]], w=[r_st5[i]])
                    for c4 in (3, 0, 2, 1):
                        cs = slice(c4 * 512, (c4 + 1) * 512)
                        if c4 in (3, 2):
                            P.op(DVE, lambda e, b=b, sc=sc, cs=cs: e.tensor_scalar(out=xo5[b][:, cs], in0=xo5[b][:, cs], scalar1=sc,
                                                                                   scalar2=None, op0=ALU.mult),
                                 r=[r_xo5p[c4], r_st5[i]], w=[r_xo5p[c4]])
                        else:
                            P.op(ACT, lambda e, b=b, sc=sc, cs=cs: e.activation(out=xo5[b][:, cs], in_=xo5[b][:, cs], func=AF.Copy,
                                                                                scale=sc),
                                 r=[r_xo5p[c4], r_st5[i]], w=[r_xo5p[c4]])
                        stores.append(P.dma(SP, y_d[i * 128:(i + 1) * 128, cs], xo5[b][:, cs], r=[r_xo5p[c4]]))
                    continue
                P.op(ACT, lambda e, b=b, sc=sc: e.activation(out=xt5[b], in_=xo5[b], func=AF.Square, accum_out=sc),
                     r=[r_xo[b]], w=[r_x5[b], r_st5[i]])
                P.op(ACT, lambda e, sc=sc: e.activation(out=sc, in_=sc, func=AF.Ln, scale=1.0 / D, bias=EPS),
                     r=[r_st5[i]], w=[r_st5[i]])
                P.op(ACT, lambda e, sc=sc: e.activation(out=sc, in_=sc, func=AF.Exp, scale=-0.5), r=[r_st5[i]], w=[r_st5[i]])
                if i + 2 < NT:
                    x5_load(i + 2)
                P.op(DVE, lambda e, b=b, sc=sc: e.scalar_tensor_tensor(out=xo5[b], in0=xo5[b], scalar=sc, in1=fgv,
                                                                       op0=ALU.mult, op1=ALU.mult),
                     r=[r_xo[b], r_st5[i], r_fg], w=[r_xo[b]])
                stores.append(P.dma(SP, y_d[i * 128:(i + 1) * 128, :], xo5[b], r=[r_xo[b]]))

        body()
        if limit < 5:
            stores.append(P.dma(SP, y_d[0:128, :], regA[:, 0:2048], r=ph0A + ph1A + cvA + p2all))
        if debug:
            if limit == 0:
                dump("hT", hT[:], [128, NDC, TOK], BF16, r_hT)
                dump("hTh", hTh[:], [128, NDC, 2], BF16, [r_hTh])
            if limit == 1:
                dump("wdec", wdec, [128, NT, 512], F32, r_wdec)
                dump("dec", dec[:], [128, 64], F32, [r_dec])
                dump("dtot", dtot[:], [128, 8], F32, [r_dec])
                dump("kdec", kdec, [128, NT, 512], BF16, r_kdec)
                dump("v", vsb, [128, NT, 1024], BF16, r_v)
                dump("qT", qT, [128, 4, 1024], BF16, r_qT)
            if limit in (3, 4):
                dump("S", S[:], [128, 4, 256], F32, r_S)
                dump("rT", rT, [128, 8, 1024], BF16, r_rT)
                dump("yT", yT[:], [128, 16, TOK], BF16, [x for l in r_yT for x in l])
            for (t, ap, res) in dbgs:
                stores.append(P.dma(SP, t, ap, r=res))
        P.emit(final_waits=stores)
    return nc, P


_CACHE = {}


def _consts():
    c = np.zeros((128, 264), np.float32)
    p = np.arange(128)
    same = (p[:, None] // 64) == (p[None, :] // 64)
    c[:, 0:128] = ((p[:, None] > p[None, :]) & same).astype(np.float32)
    c[:, 128] = (p < 64)
    c[:, 129] = (p >= 64)
    c[:, 130:258] = 1.0
    idb = np.zeros((128, 256), np.float32)
    idb[:, 0:128] = np.eye(128)
    idb[:, 128:256] = 1.0
    return c, idb.astype(ml_dtypes.bfloat16)


def kernel(x, norm_g, w_in, conv_w, conv_b, gla_w_up, gla_b_gate, gla_norm_g, w_out, final_g):
    x = np.asarray(x, np.float32)
    if "nc" not in _CACHE:
        _CACHE["nc"] = build()[0]
    nc = _CACHE["nc"]
    w_in0 = np.ascontiguousarray(np.asarray(w_in, np.float32)[0])
    w_out0 = np.ascontiguousarray(np.asarray(w_out, np.float32)[0])
    gb = np.ascontiguousarray(np.asarray(norm_g, np.float32)[0][None, :])
    fgb = np.ascontiguousarray(np.broadcast_to(np.asarray(final_g, np.float32)[None, :], (128, D)))
    pfm = np.zeros((128, 48), np.float32)
    cwv = np.asarray(conv_w, np.float32)[0]
    for j in range(3):
        pfm[:, j * 8:(j + 1) * 8] = cwv[j].reshape(8, 128).T
    pfm[:, 24:32] = np.asarray(conv_b, np.float32)[0].reshape(8, 128).T
    pfm[:, 32:40] = np.asarray(gla_norm_g, np.float32)[0].reshape(8, 128).T
    wup = np.ascontiguousarray(np.asarray(gla_w_up, np.float32)[0])
    bgt = np.ascontiguousarray(np.asarray(gla_b_gate, np.float32)[0][None, :])
    cst0, idb = _consts()
    in_maps = []
    for c in range(8):
        b, q = c // 4, c % 4
        xc = np.zeros((TOK + 2, D), np.float32)
        xc[2:] = x[b, q * TOK:(q + 1) * TOK]
        if q > 0:
            xc[0:2] = x[b, q * TOK - 2:q * TOK]
        cst = cst0.copy()
        for r in range(3):
            cst[:, 258 + r] = 1.0 if r < q else 0.0
            cst[:, 261 + r] = 0.0 if r < q else 1.0
        in_maps.append({"x": xc, "w_in": w_in0, "w_out": w_out0, "norm_g1": gb, "final_gb": fgb, "pfm": pfm,
                        "w_up": wup, "b_gate": bgt, "cst": cst, "identb": idb,
                        "identf": np.eye(128, dtype=np.float32)})
    res = run_bass_kernel_spmd(nc, in_maps, core_ids=list(range(8)))
    out = np.zeros((2, 4096, D), np.float32)
    for c in range(8):
        b, q = c // 4, c % 4
        out[b, q * TOK:(q + 1) * TOK] = res.results[c]["y"]
    return out
```

```python
import numpy as np
import ml_dtypes
from contextlib import ExitStack
import concourse.bass as bass
import concourse.mybir as mybir
from concourse.bass_utils import run_bass_kernel_spmd

F32 = mybir.dt.float32
BF16 = mybir.dt.bfloat16
F32R = mybir.dt.float32r
ALU = mybir.AluOpType
AF = mybir.ActivationFunctionType

D = 2048
TOK = 1024
NT = 8
NDC = 16
INC = 7184
EPS = 1e-6
C_H, C_B, C_C, C_Z, C_Q, C_K, C_V, C_R, C_GD = 0, 1024, 2048, 3072, 4096, 4608, 5120, 6144, 7168
ENGS = ("pe", "act", "dve", "pool", "sp")


class Res:
    __slots__ = ("name", "writers", "readers", "overlaps")

    def __init__(self, name):
        self.name = name
        self.writers = []
        self.readers = []
        self.overlaps = []


def overlap(a_list, b_list):
    for a in a_list:
        for b in b_list:
            a.overlaps.append(b)
            b.overlaps.append(a)


class Op:
    __slots__ = ("eng", "fn", "deps", "dma", "idx", "sig", "nwait", "name")

    def __init__(self, eng, fn, deps, dma, idx, name):
        self.eng, self.fn, self.deps, self.dma, self.idx, self.name = eng, fn, deps, dma, idx, name
        self.sig = None
        self.nwait = 0


class Prog:
    def __init__(self, nc, n_dma_sems=8):
        self.nc = nc
        self.ops = []
        self.n_dma_sems = n_dma_sems

    def op(self, eng, fn, r=(), w=(), deps=(), dma=False, name=None, join=False):
        dl = {}
        for d in deps:
            if d is not None:
                dl[d.idx] = d
        for res in r:
            for x in [res] + res.overlaps:
                for wr in x.writers:
                    dl[wr.idx] = wr
        for res in w:
            for x in [res] + res.overlaps:
                if not (join and x is res):
                    for wr in x.writers:
                        dl[wr.idx] = wr
                for rd in x.readers:
                    dl[rd.idx] = rd
        dlist = [d for d in dl.values() if not (eng == "pe" and d.eng == "pe")]
        o = Op(eng, fn, dlist, dma, len(self.ops), name)
        for d in dlist:
            d.nwait += 1
        for res in r:
            res.readers.append(o)
        for res in w:
            if join:
                res.writers.append(o)
            else:
                res.writers = [o]
            res.readers = []
        self.ops.append(o)
        return o

    def dma(self, eng, out, in_, r=(), w=(), deps=(), name=None, join=False):
        return self.op(eng, lambda e: e.dma_start(out=out, in_=in_), r, w, deps, dma=True, name=name, join=join)

    def emit(self, final_waits):
        nc = self.nc
        with ExitStack() as st:
            csem = {e: st.enter_context(nc.semaphore("c_" + e)) for e in ENGS}
            dsem = {e: [st.enter_context(nc.semaphore("d_%s%d" % (e, i)))
                        for i in range(self.n_dma_sems)] for e in ("sp", "pool", "act")}
            ccount = {e: 0 for e in ENGS}
            drr = {e: 0 for e in dsem}
            duse = {e: [0] * self.n_dma_sems for e in dsem}
            dprev = {}
            self.op("sp", None, deps=list(final_waits), name="final")
            for o in self.ops:
                if o.dma:
                    k = drr[o.eng]
                    drr[o.eng] = (k + 1) % self.n_dma_sems
                    s = dsem[o.eng][k]
                    if duse[o.eng][k] > 0:
                        dprev[o.idx] = (s, 16 * duse[o.eng][k])
                    duse[o.eng][k] += 1
                    o.sig = (s, 16 * duse[o.eng][k], 16)
                elif o.nwait > 0:
                    ccount[o.eng] += 1
                    o.sig = (csem[o.eng], ccount[o.eng], 1)
            per = {e: [o for o in self.ops if o.eng == e] for e in ENGS}
            self.stats = {e: len(per[e]) for e in ENGS}
            self.stats["sem_max"] = dict(ccount)

            def run(engname, eng):
                seen = {}
                for o in per[engname]:
                    waits = []
                    if o.idx in dprev:
                        waits.append(dprev[o.idx])
                    for d in o.deps:
                        assert d.sig is not None, (o.name, d.name)
                        waits.append((d.sig[0], d.sig[1]))
                    for (s, v) in waits:
                        key = id(s)
                        if seen.get(key, 0) >= v:
                            continue
                        seen[key] = v
                        eng.wait_ge(s, v)
                    if o.fn is None:
                        continue
                    ins = o.fn(eng)
                    if o.sig is not None:
                        assert ins is not None, o.name
                        ins.then_inc(o.sig[0], o.sig[2])

            with nc.Block() as block:
                @block.tensor
                def _(e):
                    run("pe", e)

                @block.scalar
                def _(e):
                    run("act", e)

                @block.vector
                def _(e):
                    run("dve", e)

                @block.gpsimd
                def _(e):
                    run("pool", e)

                @block.sync
                def _(e):
                    run("sp", e)


def build(limit=5, debug=False):
    nc = bass.Bass("TRN2", target_bir_lowering=False)
    dram_in = lambda name, shape, dt=F32: nc.dram_tensor(name, shape, dt, kind="ExternalInput").ap()
    x_d = dram_in("x", [TOK + 2, D])
    win_d = dram_in("w_in", [D, INC])
    wout_d = dram_in("w_out", [D, D])
    gb_d = dram_in("norm_g1", [1, D])
    fg_d = dram_in("final_gb", [128, D])
    pfm_d = dram_in("pfm", [128, 48])
    wup_d = dram_in("w_up", [16, 512])
    bg_d = dram_in("b_gate", [1, 512])
    cst_d = dram_in("cst", [128, 128 + 2 + 128 + 6])
    idb_d = dram_in("identb", [128, 256], BF16)
    y_d = nc.dram_tensor("y", [TOK, D], F32, kind="ExternalOutput").ap()
    cc_in = nc.dram_tensor("cc_in", [128, 1028], F32)
    cc_out = nc.dram_tensor("cc_out", [4 * 128, 1028], F32)

    P = Prog(nc)
    st = ExitStack()
    with st:
        def sb(name, shape, dt):
            return st.enter_context(nc.sbuf_tensor("s_" + name, shape, dt))

        cst = sb("cst", [128, 264], F32)
        idb = sb("idb", [128, 256], BF16)
        pfm = sb("pfm", [128, 48], F32)
        wupr = sb("wupr", [33, 512], F32R)
        maskUr = sb("maskUr", [128, 128], F32R)
        stat = sb("stat", [128, 64], F32)
        dec = sb("dec", [128, 64], F32)
        bpre = sb("bpre", [128, 64], F32)
        dtot = sb("dtot", [128, 8], F32)
        hal = sb("hal", [128, 4], F32)
        regA = sb("regA", [128, 8192], F32)
        hT = sb("hT", [128, NDC, TOK], BF16)
        hTh = sb("hTh", [128, NDC, 2], BF16)
        NSL = 3
        wsl = [sb("wsl%d" % i, [128, NDC, 256], BF16) for i in range(NSL)]
        big = sb("big", [128, 16384], F32)
        yT = sb("yT", [128, 16, TOK], BF16)
        S = sb("S", [128, 4, 256], F32)
        Sbf = sb("Sbf", [128, 2, 4, 256], BF16)

        maskU = cst[:, 0:128]
        ind2 = cst[:, 128:130]
        ones_f = cst[:, 130:258]
        msk = cst[:, 258:261]
        omsk = cst[:, 261:264]
        ident = idb[:, 0:128]
        ones_b = idb[:, 128:256]
        cw = lambda j, g: pfm[:, j * 8 + g: j * 8 + g + 1]
        cbias = lambda g: pfm[:, 24 + g: 25 + g]
        gn = lambda hj: pfm[:, 32 + hj: 33 + hj]

        bigb = big[:].bitcast(BF16)
        kdec = bigb[:, 0:4096].rearrange("p (i c) -> p i c", i=NT)
        vsb = bigb[:, 4096:12288].rearrange("p (i c) -> p i c", i=NT)
        qT = bigb[:, 12288:16384].rearrange("p (h t) -> p h t", h=4)
        rT = bigb[:, 16384:24576].rearrange("p (g t) -> p g t", g=8)
        gdT = sb("gdTr", [33, 1024], F32R)
        lr = sb("lr", [128, 2, 512], F32R)
        stg = regA[:, 7168:8192]
        stgB = regA[:, 6144:6656]
        woutsb = bigb.rearrange("p (m c) -> p m c", m=16)
        Ab = regA[:].bitcast(BF16)
        kdecO = Ab[:, 12288:16384].rearrange("p (i c) -> p i c", i=NT)
        NXR = 4
        yTf = yT[:].rearrange("p m t -> p (m t)").bitcast(F32)
        xt0 = [yTf[:, k * 2048:(k + 1) * 2048] for k in range(NXR)]
        xs0 = [Ab[:, 0:2048], Ab[:, 2048:4096]]
        junk0 = Ab[:, 4096:6144]
        gbsb = regA[:, 3072:5120]
        g1 = big[:, 0:2048]
        e_sb = [regA[:, 5120:5632], regA[:, 5632:6144]]
        l_sb = [lr[:, 0, :], lr[:, 1, :]]
        wdec = regA[:, 0:4096].rearrange("p (i c) -> p i c", i=NT)
        gat = regA[:, 0:3 * 1028].rearrange("p (r c) -> p r c", r=3)
        Csb = regA[:, 0:1024]
        usb = regA[:, 1024:2050]
        ysb = regA[:, 2052:3076]
        szb = Ab[:, 6160:7184]
        def p2tmp(k):
            b = 4096 + k * 1024
            sq = Ab[:, 2 * b:2 * b + 512].rearrange("p (a t) -> p a t", a=8)
            rs = regA[:, b + 256:b + 512].rearrange("p (h t) -> p h t", h=4)
            y1 = regA[:, b + 512:b + 1024].rearrange("p (a t) -> p a t", a=8)
            return sq, rs, y1
        xt5 = [regA[:, 0:2048], regA[:, 2048:4096]]
        xo5 = [regA[:, 4096:6144], regA[:, 6144:8192]]

        SP, ACT, DVE, PE, POOL = "sp", "act", "dve", "pe", "pool"
        R = lambda n: Res(n)
        r_cst = R("cst")
        r_xt = [R("xt%d" % k) for k in range(NXR)]
        r_xs = [R("xs0"), R("xs1")]
        r_junk = R("junk")
        r_gb = R("gb")
        r_g1 = R("g1")
        r_st0 = [R("st0_%d" % i) for i in range(9)]
        r_st5 = [R("st5_%d" % i) for i in range(NT)]
        r_dm = R("dm")
        r_dummy = R("dummy")
        r_hT = [R("hT%d" % i) for i in range(NT)]
        r_hTh = R("hTh")
        r_w = [R("wsl%d" % i) for i in range(NSL)]
        r_kdec = [R("kdec%d" % i) for i in range(NT)]
        r_kz = R("kdec_zero")
        r_kdec2 = [[R("kdec%d_%d" % (i, b)) for b in range(2)] for i in range(NT)]
        r_v = [R("v%d" % i) for i in range(NT)]
        r_qT = [R("qT%d" % h) for h in range(4)]
        r_gdT = R("gdT")
        r_stg = R("stg")
        r_stgB = R("stgB")
        r_wupr = R("wupr")
        r_rT = [R("rT%d" % g) for g in range(8)]
        r_e = [R("e0"), R("e1")]
        r_l = [R("l0"), R("l1")]
        r_wdec = [R("wdec%d" % i) for i in range(NT)]
        r_dec = R("dec")
        r_gat = R("gat")
        r_S = [R("S%d" % h) for h in range(4)]
        r_Sbf = [[R("Sbf%d_%d" % (b, h)) for h in range(4)] for b in range(2)]
        r_yT = [[R("yT%d_%d" % (m, i)) for i in range(NT)] for m in range(16)]
        r_cv = [R("Csb"), R("usb"), R("ysb"), R("szb"), R("hal")]
        r_p2 = [[R("p2_%d_%d" % (k, j)) for j in range(3)] for k in range(2)]
        r_x5 = [R("x5_0"), R("x5_1")]
        r_xo = [R("xo0"), R("xo1")]
        r_xo5p = [R("xo5p%d" % c) for c in range(4)]
        r_wout = [R("wout%d" % c) for c in range(8)]
        r_fg = R("fg")
        r_cc = R("cc")
        p2all = r_p2[0] + r_p2[1]
        ph0A = r_xs + [r_junk, r_gb]
        ph1A = r_e + r_l + r_wdec + [r_gat]
        cvA = r_cv[0:4]
        ph5A = r_x5 + r_xo
        overlap(ph1A, ph0A)
        overlap([r_gat], r_wdec)
        overlap(cvA, ph0A + ph1A)
        overlap(p2all, ph0A + ph1A)
        overlap(ph5A, ph0A + ph1A + cvA + p2all)
        overlap(r_xt, [x for l in r_yT for x in l])
        overlap(r_wout, r_kdec + r_v + r_qT + [r_gdT] + r_rT)
        overlap([r_fg], r_w)
        overlap([r_stg, r_stgB], ph5A)
        overlap(r_xo5p, [r_xo[1]])
        overlap([r_g1], r_kdec + r_wout)
        allk2 = [x for l in r_kdec2 for x in l]
        overlap(allk2 + r_kdec + [r_kz], [r_stg, r_stgB] + ph5A)
        overlap(allk2, r_wout + [r_g1])

        pbank = [st.enter_context(nc.psum_tensor("pb%d" % i, [128, 512], F32)) for i in range(8)]
        r_pb = [R("pb%d" % i) for i in range(8)]
        r_mbh = R("mb_halo")
        r_mbs = R("mb_ss")

        stores = []
        dbgs = []

        def dump(name, ap, shape, dt, res):
            t = nc.dram_tensor("dbg_" + name, shape, dt, kind="ExternalOutput").ap()
            dbgs.append((t, ap, res))

        def body():
            P.op(DVE, lambda e: e.memset(stat[:, 56:64], 1.0), w=[r_dummy])
            def load_consts():
                P.dma(SP, cst[:], cst_d, w=[r_cst])
                P.dma(SP, idb[:], idb_d, w=[r_cst], join=True)
                P.dma(SP, pfm[:], pfm_d, w=[r_cst], join=True)

            wcols = lambda c0, n: win_d[:, c0:c0 + n]
            wconv = win_d[:, 0:4096].rearrange("(dc p) (s t g j) -> p dc s t g j", p=128, s=2, t=2, g=8, j=128)
            blk256 = lambda c0: [(wcols(c0, 256).rearrange("(dc p) c -> p dc c", p=128), 0, 256)]
            wplan = [blk256(C_V)]
            wplan += [[(wcols(C_GD, 16).rearrange("(dc p) c -> p dc c", p=128), 0, 16)]]
            wplan += [blk256(C_V + b * 256) for b in range(1, 4)]
            wplan += [blk256(C_K + b * 256) for b in range(2)]
            wplan += [blk256(C_Q + b * 256) for b in range(2)]
            wplan += [blk256(C_R + b * 256) for b in range(4)]
            for g in range(8):
                for t in range(2):
                    wplan.append([(wconv[:, :, sg, t, g, :], sg * 128, 128) for sg in range(2)])
            wst = {"loaded": 0, "used": 0}

            def w_emit(deps=()):
                n = wst["loaded"]
                if n >= len(wplan):
                    return
                wst["loaded"] += 1
                s = n % NSL
                op = None
                for k, (src, c0, nn) in enumerate(wplan[n]):
                    op = P.dma(POOL, wsl[s][:, :, c0:c0 + nn], src, w=[r_w[s]], join=(k > 0), deps=deps)
                return op

            def wload():
                n = wst["used"]
                wst["used"] += 1
                assert n < wst["loaded"]
                return n % NSL

            def wdone(k=1):
                for _ in range(k):
                    w_emit()

            def x_load(i, deps=()):
                if i < NT:
                    return P.dma(SP, xt0[i % NXR], x_d[2 + i * 128: 2 + (i + 1) * 128, :], w=[r_xt[i % NXR]], deps=deps)
                return P.dma(SP, xt0[i % NXR][0:2, :], x_d[0:2, :], w=[r_xt[i % NXR]])

            def st1(i, npart):
                xk = i % NXR
                sc = stat[0:npart, i:i + 1]
                P.op(ACT, lambda e: e.activation(out=junk0[0:npart, :], in_=xt0[xk][0:npart, :], func=AF.Square,
                                                 accum_out=sc), r=[r_xt[xk]], w=[r_junk, r_st0[i]])
                P.op(ACT, lambda e: e.activation(out=sc, in_=sc, func=AF.Ln, scale=1.0 / D, bias=EPS),
                     r=[r_st0[i]], w=[r_st0[i]])
                P.op(ACT, lambda e: e.activation(out=sc, in_=sc, func=AF.Exp, scale=-0.5), r=[r_st0[i]], w=[r_st0[i]])

            def st2(i, npart):
                b = i % 2
                xk = i % NXR
                sc = stat[0:npart, i:i + 1]
                P.op(DVE, lambda e: e.scalar_tensor_tensor(out=xs0[b][0:npart, :], in0=xt0[xk][0:npart, :], scalar=sc,
                                                           in1=gbsb[0:npart, :], op0=ALU.mult, op1=ALU.mult),
                     r=[r_xt[xk], r_st0[i], r_gb], w=[r_xs[b]])

            def st3(i):
                if i == NT:
                    pvh = pbank[0][:].bitcast(BF16)

                    def trh(e):
                        ins = None
                        for dc in range(16):
                            ins = e.transpose(out=pvh[:, dc * 2:dc * 2 + 2], in_=xs0[0][0:2, dc * 128:(dc + 1) * 128],
                                              identity=ident[0:2, 0:2])
                        return ins
                    P.op(PE, trh, r=[r_xs[0], r_cst], w=[r_pb[0]])
                    P.op(DVE, lambda e: e.tensor_copy(out=hTh[:], in_=pvh[:, 0:32].rearrange("p (d t) -> p d t", d=16)),
                         r=[r_pb[0]], w=[r_hTh])
                    return
                for half in range(2):
                    bk = (2 * i + half) % 4
                    pv = pbank[bk][:].bitcast(BF16)

                    def tr(e, half=half, pv=pv):
                        ins = None
                        for j in range(8):
                            dc = half * 8 + j
                            ins = e.transpose(out=pv[:, j * 128:(j + 1) * 128],
                                              in_=xs0[i % 2][:, dc * 128:(dc + 1) * 128], identity=ident)
                        return ins
                    P.op(PE, tr, r=[r_xs[i % 2], r_cst], w=[r_pb[bk]])
                    dst = hT[:, half * 8:(half + 1) * 8, i * 128:(i + 1) * 128]
                    srcv = pv.rearrange("p (j t) -> p j t", j=8)
                    if half == 0:
                        P.op(ACT, lambda e, dst=dst, srcv=srcv: e.activation(out=dst, in_=srcv, func=AF.Copy),
                             r=[r_pb[bk]], w=[r_hT[i]])
                    else:
                        P.op(DVE, lambda e, dst=dst, srcv=srcv: e.tensor_copy(out=dst, in_=srcv),
                             r=[r_pb[bk]], w=[r_hT[i]], join=True)

            ring = {"n": 0, "k": 6}

            def nb():
                b = ring["n"] % ring["k"]
                ring["n"] += 1
                return b

            def featmajor(s, col0, halo_bank=None, halo_col=0, mid=None):
                ba, bb = nb(), nb()
                first = 4 if (mid is not None and halo_bank is not None) else 0
                last = (first - 1) % NDC

                def mm(e, dcs, halo):
                    ins = None
                    for dc in dcs:
                        lw = wsl[s][:, dc, col0:col0 + 128]
                        ins = e.matmul(pbank[ba][:], lhsT=lw, rhs=hT[:, dc, 0:512], start=(dc == 0), stop=(dc == NDC - 1))
                        ins = e.matmul(pbank[bb][:], lhsT=lw, rhs=hT[:, dc, 512:1024], start=(dc == 0), stop=(dc == NDC - 1))
                        if halo:
                            ins = e.matmul(pbank[halo_bank][:, halo_col:halo_col + 2], lhsT=lw, rhs=hTh[:, dc, :],
                                           start=(dc == first), stop=(dc == last))
                    return ins

                def halo_only(e, dcs):
                    ins = None
                    for dc in dcs:
                        ins = e.matmul(pbank[halo_bank][:, halo_col:halo_col + 2], lhsT=wsl[s][:, dc, col0:col0 + 128],
                                       rhs=hTh[:, dc, :], start=(dc == first), stop=(dc == last))
                    return ins
                hb = halo_bank is not None
                wr = [r_pb[ba], r_pb[bb]]
                rd = [r_w[s], r_hTh] + r_hT
                if mid is None:
                    P.op(PE, lambda e: mm(e, range(NDC), hb), r=rd, w=wr + ([r_pb[halo_bank]] if hb else []))
                else:
                    P.op(PE, lambda e: mm(e, range(0, 4), False), r=rd, w=wr)
                    mid()
                    if hb:
                        P.op(PE, lambda e: (mm(e, range(4, NDC), True), halo_only(e, range(0, 4)))[1], r=rd,
                             w=wr + [r_pb[halo_bank]])
                    else:
                        P.op(PE, lambda e: mm(e, range(4, NDC), False), r=rd, w=wr)
                return ba, bb

            def tok_unit(s, i, evac, bk=None):
                if bk is None:
                    bk = nb()

                def mm(e):
                    ins = None
                    for dc in range(NDC):
                        ins = e.matmul(pbank[bk][:, 0:256], lhsT=hT[:, dc, i * 128:(i + 1) * 128],
                                       rhs=wsl[s][:, dc, :], start=(dc == 0), stop=(dc == NDC - 1))
                    return ins
                P.op(PE, mm, r=[r_w[s], r_hT[i]], w=[r_pb[bk]])
                evac(i, bk)

            P.dma(SP, g1[0:1, :], gb_d, w=[r_g1])
            load_consts()
            x0op = x_load(0)
            w0op = w_emit()
            P.op(DVE, lambda e: e.memset(stg[0:64, :], 0.0), w=[r_stg])
            P.op(DVE, lambda e: e.memset(stg[32:33, :], 1.0), w=[r_stg])
            P.op(DVE, lambda e: e.memset(stgB[0:64, :], 0.0), w=[r_stgB])
            for c in range(4):
                P.op(PE, lambda e, c=c: e.matmul(pbank[4 + c][:], lhsT=ones_f[0:1, :], rhs=g1[0:1, c * 512:(c + 1) * 512],
                                                 start=True, stop=True), r=[r_cst, r_g1], w=[r_pb[4 + c]])
                P.op(DVE, lambda e, c=c: e.tensor_copy(out=gbsb[:, c * 512:(c + 1) * 512], in_=pbank[4 + c][:]),
                     r=[r_pb[4 + c]], w=[r_gb], join=(c > 0))
            x_load(1, deps=[x0op])
            x_load(2, deps=[w0op])
            x_load(3, deps=[w0op])
            P.dma(SP, stgB[0:16, :], wup_d, w=[r_stgB])
            P.dma(SP, stgB[32:33, :], bg_d, w=[r_stgB], join=True)

            def make_evac_v(blk):
                def evac_v(i, bk):
                    if (i + blk) % 2 == 0:
                        P.op(ACT, lambda e: e.activation(out=vsb[:, i, blk * 256:(blk + 1) * 256],
                                                         in_=pbank[bk][:, 0:256], func=AF.Copy),
                             r=[r_pb[bk]], w=[r_v[i]])
                    else:
                        P.op(DVE, lambda e: e.tensor_copy(out=vsb[:, i, blk * 256:(blk + 1) * 256],
                                                          in_=pbank[bk][:, 0:256]), r=[r_pb[bk]], w=[r_v[i]])
                return evac_v

            s_v0 = wload()
            npt = lambda i: 128 if i < NT else 2
            V0LAG = 3
            for t in range(NT + V0LAG):
                if t <= NT:
                    st1(t, npt(t))
                if 0 <= t - 1 <= NT:
                    st2(t - 1, npt(t - 1))
                if 4 <= t + 3 <= NT:
                    xop = x_load(t + 3)
                    if t + 3 == NT - 1:
                        for _ in range(NSL - 1):
                            w_emit(deps=[xop])
                if 0 <= t - 2 <= NT:
                    st3(t - 2)
                if 0 <= t - V0LAG < NT:
                    tok_unit(s_v0, t - V0LAG, make_evac_v(0), bk=4 + (t - V0LAG) % 4)
            wdone()
            s_gd = wload()
            if limit < 1:
                return

            P.op(ACT, lambda e: e.activation(out=gdT[0:33, :], in_=stg[0:33, :], func=AF.Copy), r=[r_stg], w=[r_gdT])
            P.op(ACT, lambda e: e.activation(out=wupr[0:33, :], in_=stgB[0:33, :], func=AF.Copy), r=[r_stgB], w=[r_wupr])
            P.op(ACT, lambda e: e.activation(out=maskUr[:], in_=maskU, func=AF.Copy), r=[r_cst], w=[r_wupr], join=True)
            P.op(POOL, lambda e: e.memset(kdecO[0:64, :, :], 0.0), r=[r_stg, r_stgB], w=[r_kz] + r_kdec)
            b0, b1 = nb(), nb()

            def gdmm(e, s=s_gd):
                ins = None
                for dc in range(NDC):
                    for tb, bk in ((0, b0), (1, b1)):
                        ins = e.matmul(pbank[bk][0:16, :], lhsT=wsl[s][:, dc, 0:16], rhs=hT[:, dc, tb * 512:(tb + 1) * 512],
                                       start=(dc == 0), stop=(dc == NDC - 1))
                return ins
            P.op(PE, gdmm, r=[r_w[s_gd]] + r_hT, w=[r_pb[b0], r_pb[b1]])
            wdone()
            P.op(ACT, lambda e: e.activation(out=gdT[0:16, 0:512], in_=pbank[b0][0:16, :], func=AF.Copy),
                 r=[r_pb[b0]], w=[r_gdT])
            P.op(ACT, lambda e: e.activation(out=gdT[0:16, 512:1024], in_=pbank[b1][0:16, :], func=AF.Copy),
                 r=[r_pb[b1]], w=[r_gdT])

            def gen_v():
                for blk in range(1, 4):
                    s = wload()
                    for i in range(NT):
                        tok_unit(s, i, make_evac_v(blk))
                        if i == NT - 1:
                            wdone()
                        yield
            gv = gen_v()
            next(gv)
            next(gv)

            for i in range(NT):
                j = i % 2
                bx = nb()
                P.op(PE, lambda e, i=i, bx=bx: e.matmul(pbank[bx][:], lhsT=gdT[0:33, i * 128:(i + 1) * 128],
                                                        rhs=wupr[0:33, :], start=True, stop=True),
                     r=[r_gdT, r_wupr], w=[r_pb[bx]])
                P.op(ACT, lambda e, j=j, bx=bx: e.activation(out=e_sb[j], in_=pbank[bx][:], func=AF.Exp, scale=-1.0),
                     r=[r_pb[bx]], w=[r_e[j]])
                P.op(ACT, lambda e, j=j: e.activation(out=l_sb[j], in_=e_sb[j], func=AF.Ln, bias=1.0),
                     r=[r_e[j]], w=[r_l[j]])
                next(gv)
                next(gv)
                brv, bbe = nb(), nb()
                P.op(PE, lambda e, j=j, brv=brv: e.matmul(pbank[brv][:], lhsT=maskUr[:], rhs=l_sb[j],
                                                          start=True, stop=True),
                     r=[r_l[j], r_wupr], w=[r_pb[brv]])
                P.op(ACT, lambda e, i=i, brv=brv: e.activation(out=wdec[:, i, :], in_=pbank[brv][:], func=AF.Exp,
                                                               scale=-1.0 / 16.0),
                     r=[r_pb[brv]], w=[r_wdec[i]])

                def bem(e, j=j, bbe=bbe):
                    ins = None
                    for h in range(4):
                        ins = e.matmul(pbank[bbe][:, h * 2:h * 2 + 2], lhsT=l_sb[j][:, h * 128:(h + 1) * 128].bitcast(F32), rhs=ind2,
                                       start=True, stop=True)
                    return ins
                P.op(PE, bem, r=[r_l[j], r_cst], w=[r_pb[bbe]])
                P.op(DVE, lambda e, i=i, bbe=bbe: e.tensor_copy(out=bpre[:, i * 8:(i + 1) * 8], in_=pbank[bbe][:, 0:8]),
                     r=[r_pb[bbe]], w=[r_dec])
                if i < 6:
                    next(gv)
            for _ in gv:
                pass
            P.op(ACT, lambda e: e.activation(out=dec[:], in_=bpre[:], func=AF.Exp, scale=-1.0 / 16.0), r=[r_dec], w=[r_dec])
            P.op(DVE, lambda e: e.tensor_reduce(out=dtot[:, 4:8], in_=bpre[:].rearrange("p (i h s) -> p h i s", i=8, h=4),
                                                axis=mybir.AxisListType.XY, op=ALU.add), r=[r_dec], w=[r_dec])
            P.op(ACT, lambda e: e.activation(out=dtot[:, 0:4], in_=dtot[:, 4:8], func=AF.Exp, scale=-1.0 / 16.0),
                 r=[r_dec], w=[r_dec])

            UB1 = [6, 7]

            def p1_chunk(n):
                i, s_ = n // 2, n % 2
                ps0, ps1 = s_ * 64, s_ * 64 + 64
                for h in range(4):
                    ub, uo = UB1[h // 2], (h % 2) * 256
                    P.op(PE, lambda e, i=i, h=h, ub=ub, uo=uo: e.matmul(
                        pbank[ub][:, uo:uo + 256], lhsT=(kdec, kdecO)[s_][:, i, h * 128:(h + 1) * 128],
                        rhs=vsb[:, i, h * 256:(h + 1) * 256], start=True, stop=True),
                        r=[r_kdec2[i][h // 2], r_v[i], r_kz], w=[r_pb[ub]])
                for h in range(4):
                    ub, uo = UB1[h // 2], (h % 2) * 256
                    P.op(DVE, lambda e, i=i, h=h, ub=ub, uo=uo: e.scalar_tensor_tensor(
                        out=S[:, h, :], in0=S[:, h, :], scalar=dec[:, i * 8 + h * 2 + s_: i * 8 + h * 2 + s_ + 1],
                        in1=pbank[ub][:, uo:uo + 256], op0=ALU.mult, op1=ALU.add),
                        r=[r_S[h], r_dec, r_pb[ub]], w=[r_S[h]])

            ring["n"] = 0
            P.op(DVE, lambda e: e.memset(S[:], 0.0), w=r_S)
            s_k = [wload(), wload()]
            for i in range(NT):
                for blk in range(2):
                    def evac_k(i, bk, blk=blk):
                        cs = slice(blk * 256, (blk + 1) * 256)
                        rk = r_kdec2[i][blk]
                        P.op(DVE, lambda e: e.tensor_tensor(out=kdec[:, i, cs], in0=pbank[bk][:, 0:256],
                                                            in1=wdec[:, i, cs], op=ALU.mult),
                             r=[r_pb[bk], r_wdec[i]], w=[rk])
                        P.op(POOL, lambda e: e.tensor_copy(out=kdecO[64:128, i, cs], in_=kdec[64:128, i, cs]),
                             r=[rk, r_kz], w=[rk])
                        P.op(POOL, lambda e: e.memset(kdec[64:128, i, cs], 0.0), w=[rk])
                    tok_unit(s_k[blk], i, evac_k)
                    if i >= 1:
                        p1_chunk(2 * (i - 1) + blk)
                if i == NT - 1:
                    wdone(2)

            def q_group(blk, hh, s):
                h = blk * 2 + hh
                ba, bb = featmajor(s, hh * 128)
                for tb, bk in ((0, ba), (1, bb)):
                    P.op(ACT, lambda e, h=h, tb=tb, bk=bk: e.activation(out=qT[:, h, tb * 512:(tb + 1) * 512],
                                                                        in_=pbank[bk][:], func=AF.Copy,
                                                                        scale=float(128 ** -0.5)),
                         r=[r_pb[bk]], w=[r_qT[h]])
            s_q0 = wload()
            q_group(0, 0, s_q0)
            p1_chunk(14)
            p1_chunk(15)
            if limit >= 2:
                P.dma(SP, cc_in.ap()[:, 0:1024], S[:].rearrange("p h v -> p (h v)"), r=r_S, w=[r_cc])
                P.dma(SP, cc_in.ap()[:, 1024:1028], dtot[:, 0:4], r=[r_dec], w=[r_cc])
            q_group(0, 1, s_q0)
            wdone()
            s_q1 = wload()
            if limit >= 2:
                P.op(POOL, lambda e: e.collective_compute("AllGather", ALU.bypass,
                                                          replica_groups=[[0, 1, 2, 3], [4, 5, 6, 7]],
                                                          ins=[cc_in.ap().opt()], outs=[cc_out.ap().opt()]),
                     r=[r_cc], w=[r_cc])
            q_group(1, 0, s_q1)
            q_group(1, 1, s_q1)
            wdone()
            if limit < 2:
                return

            def combine():
                P.dma(SP, gat, cc_out.ap()[0:384, :].rearrange("(r p) c -> p r c", p=128), r=[r_cc], w=[r_gat])
                P.op(DVE, lambda e: e.memset(S[:], 0.0), w=r_S)
                dm = stat[:, 16:28]
                for r_ in range(3):
                    P.op(DVE, lambda e, r_=r_: e.tensor_scalar(out=dm[:, r_ * 4:(r_ + 1) * 4], in0=gat[:, r_, 1024:1028],
                                                               scalar1=msk[:, r_:r_ + 1], scalar2=omsk[:, r_:r_ + 1],
                                                               op0=ALU.mult, op1=ALU.add), r=[r_gat, r_cst], w=[r_dm])
                    P.op(DVE, lambda e, r_=r_: e.tensor_scalar(out=gat[:, r_, 0:1024], in0=gat[:, r_, 0:1024],
                                                               scalar1=msk[:, r_:r_ + 1], scalar2=None, op0=ALU.mult),
                         r=[r_gat, r_cst], w=[r_gat])
                    for h in range(4):
                        P.op(DVE, lambda e, r_=r_, h=h: e.scalar_tensor_tensor(
                            out=S[:, h, :], in0=S[:, h, :], scalar=dm[:, r_ * 4 + h:r_ * 4 + h + 1],
                            in1=gat[:, r_, h * 256:(h + 1) * 256], op0=ALU.mult, op1=ALU.add),
                            r=[r_gat, r_dm, r_S[h]], w=[r_S[h]])

            for blk in range(4):
                s = wload()
                for hh in range(2):
                    g = blk * 2 + hh
                    ba, bb = featmajor(s, hh * 128)
                    for tb, bk in ((0, ba), (1, bb)):
                        P.op(ACT, lambda e, g=g, tb=tb, bk=bk: e.activation(out=rT[:, g, tb * 512:(tb + 1) * 512],
                                                                            in_=pbank[bk][:], func=AF.Silu),
                             r=[r_pb[bk]], w=[r_rT[g]])
                    P.op(DVE, lambda e, g=g: e.tensor_scalar(out=rT[:, g, :], in0=rT[:, g, :], scalar1=gn(g), scalar2=None,
                                                             op0=ALU.mult), r=[r_rT[g], r_cst], w=[r_rT[g]])
                    if hh == 1:
                        wdone()
                        if blk == 2:
                            combine()

            if debug:
                dump("Sin", S[:], [128, 4, 256], F32, r_S)
                for (t, ap, res) in dbgs:
                    stores.append(P.dma(SP, t, ap, r=res))
                del dbgs[:]
            if limit < 3:
                return

            MB = 4
            UB2 = 5
            OB = [6, 7]

            def p2_A(n, hp):
                i, s_ = n // 2, n % 2
                ps0, ps1 = s_ * 64, s_ * 64 + 64
                bf = n % 2
                for hh in range(2):
                    h = hp * 2 + hh
                    uo = hh * 256
                    P.op(PE, lambda e, i=i, h=h, uo=uo: e.matmul(
                        pbank[UB2][:, uo:uo + 256], lhsT=(kdec, kdecO)[s_][:, i, h * 128:(h + 1) * 128],
                        rhs=vsb[:, i, h * 256:(h + 1) * 256], start=True, stop=True),
                        r=[r_kdec2[i][h // 2], r_v[i], r_kz], w=[r_pb[UB2]])
                for hh in range(2):
                    h = hp * 2 + hh
                    uo = hh * 256
                    P.op(DVE, lambda e, i=i, h=h, uo=uo: e.scalar_tensor_tensor(
                        out=S[:, h, :], in0=S[:, h, :], scalar=dec[:, i * 8 + h * 2 + s_: i * 8 + h * 2 + s_ + 1],
                        in1=pbank[UB2][:, uo:uo + 256], op0=ALU.mult, op1=ALU.add),
                        r=[r_S[h], r_dec, r_pb[UB2]], w=[r_S[h]])
                    P.op(ACT, lambda e, h=h, bf=bf: e.activation(out=Sbf[:, bf, h, :], in_=S[:, h, :], func=AF.Copy),
                         r=[r_S[h]], w=[r_Sbf[bf][h]])

            def p2_B(n):
                bf = n % 2
                ob = OB[n % 2]

                def om(e):
                    ins = None
                    for h in range(4):
                        for j in range(2):
                            hj = h * 2 + j
                            ins = e.matmul(pbank[ob][:, hj * 64:(hj + 1) * 64], lhsT=Sbf[:, bf, h, j * 128:(j + 1) * 128],
                                           rhs=qT[:, h, n * 64:(n + 1) * 64], start=True, stop=True)
                    return ins
                P.op(PE, om, r=r_Sbf[bf] + r_qT, w=[r_pb[ob]])
                sq, rs, y1 = p2tmp(n % 2)
                rp = r_p2[n % 2]
                P.op(ACT, lambda e: e.activation(out=sq, in_=pbank[ob][:].rearrange("p (a t) -> p a t", a=8),
                                                 func=AF.Square), r=[r_pb[ob]], w=[rp[0]])

            def p2_C(n):
                ob = OB[n % 2]
                sq, rs, y1 = p2tmp(n % 2)
                rp = r_p2[n % 2]

                def ssm(e):
                    ins = None
                    for h in range(4):
                        for j in range(2):
                            ins = e.matmul(pbank[MB][:, 256 + h * 64:256 + (h + 1) * 64], lhsT=ones_b, rhs=sq[:, h * 2 + j, :],
                                           start=(j == 0), stop=(j == 1))
                    return ins
                P.op(PE, ssm, r=[rp[0], r_cst], w=[r_pb[MB]])
                P.op(ACT, lambda e: e.activation(out=rs, in_=pbank[MB][:, 256:512].rearrange("p (h t) -> p h t", h=4),
                                                 func=AF.Ln, scale=1.0 / 256.0, bias=EPS), r=[r_pb[MB]], w=[rp[1]])
                P.op(ACT, lambda e: e.activation(out=rs, in_=rs, func=AF.Exp, scale=-0.5), r=[rp[1]], w=[rp[1]])
                P.op(DVE, lambda e: e.tensor_tensor(
                    out=y1.rearrange("p (h j) t -> p h j t", h=4),
                    in0=pbank[ob][:].rearrange("p (h j t) -> p h j t", h=4, j=2),
                    in1=rs.unsqueeze(2).to_broadcast([128, 4, 2, 64]), op=ALU.mult),
                    r=[r_pb[ob], rp[1]], w=[rp[2]])
                i = n // 2
                P.op(DVE, lambda e: e.tensor_tensor(out=yT[:, 8:16, n * 64:(n + 1) * 64], in0=y1,
                                                    in1=rT[:, :, n * 64:(n + 1) * 64], op=ALU.mult),
                     r=[rp[2]] + r_rT, w=[r_yT[m][i] for m in range(8, 16)])

            def p2_pre(k):
                if 0 <= k - 2 < 16:
                    p2_B(k - 2)

            def p2_mid(k):
                if 0 <= k < 16:
                    p2_A(k, 0)

            def p2_post(k):
                if 0 <= k < 16:
                    p2_A(k, 1)
                if 0 <= k - 2 < 16:
                    p2_C(k - 2)

            ring["n"] = 0
            ring["k"] = 4
            unit = {"k": 0}

            P2LAG = 1

            def side():
                p2_pre(unit["k"] - P2LAG)

            def mid():
                p2_mid(unit["k"] - P2LAG)

            def post():
                p2_post(unit["k"] - P2LAG)
                unit["k"] += 1

            WSCHED = {5: (2, 3, 4), 6: (5, 6, 7)}
            WSCHED_END = {4: (0, 1)}
            for g in range(8):
                s1 = wload()
                hc = g * 4
                side()
                ca, cbk = featmajor(s1, 128, MB, hc, mid=mid)
                P.op(ACT, lambda e, hc=hc: e.activation(out=hal[:, 0:2], in_=pbank[MB][:, hc:hc + 2], func=AF.Copy),
                     r=[r_pb[MB]], w=[r_cv[4]])
                post()
                P.op(ACT, lambda e, ca=ca: e.activation(out=Csb[:, 0:512], in_=pbank[ca][:], func=AF.Copy),
                     r=[r_pb[ca]], w=[r_cv[0]])
                P.op(ACT, lambda e, cbk=cbk: e.activation(out=Csb[:, 512:1024], in_=pbank[cbk][:], func=AF.Copy),
                     r=[r_pb[cbk]], w=[r_cv[0]])
                side()
                ha, hb_ = featmajor(s1, 0, MB, hc + 2, mid=mid)
                P.op(ACT, lambda e, hc=hc: e.activation(out=hal[:, 2:4], in_=pbank[MB][:, hc + 2:hc + 4], func=AF.Copy),
                     r=[r_pb[MB]], w=[r_cv[4]], join=True)
                post()
                P.op(DVE, lambda e: e.tensor_tensor(out=usb[:, 0:2], in0=hal[:, 2:4], in1=hal[:, 0:2], op=ALU.mult),
                     r=[r_cv[4]], w=[r_cv[1]])
                P.op(DVE, lambda e, ha=ha: e.tensor_tensor(out=usb[:, 2:514], in0=pbank[ha][:], in1=Csb[:, 0:512],
                                                           op=ALU.mult), r=[r_pb[ha], r_cv[0]], w=[r_cv[1]])
                P.op(DVE, lambda e, hb_=hb_: e.tensor_tensor(out=usb[:, 514:1026], in0=pbank[hb_][:], in1=Csb[:, 512:1024],
                                                             op=ALU.mult), r=[r_pb[hb_], r_cv[0]], w=[r_cv[1]])
                P.op(DVE, lambda e, g=g: e.tensor_scalar(out=ysb, in0=usb[:, 2:1026], scalar1=cw(2, g),
                                                         scalar2=cbias(g), op0=ALU.mult, op1=ALU.add),
                     r=[r_cv[1], r_cst], w=[r_cv[2]])
                P.op(DVE, lambda e, g=g: e.scalar_tensor_tensor(out=ysb, in0=usb[:, 1:1025], scalar=cw(1, g),
                                                                in1=ysb, op0=ALU.mult, op1=ALU.add),
                     r=[r_cv[1], r_cv[2], r_cst], w=[r_cv[2]])
                P.op(DVE, lambda e, g=g: e.scalar_tensor_tensor(out=ysb, in0=usb[:, 0:1024], scalar=cw(0, g),
                                                                in1=ysb, op0=ALU.mult, op1=ALU.add),
                     r=[r_cv[1], r_cv[2], r_cst], w=[r_cv[2]])
                wdone()
                if limit >= 5 and g in WSCHED:
                    for cb in WSCHED[g]:
                        P.dma(POOL, woutsb[:, :, cb * 256:(cb + 1) * 256],
                              wout_d[:, cb * 256:(cb + 1) * 256].rearrange("(m p) c -> p m c", p=128), w=[r_wout[cb]])
                s2 = wload()
                side()
                Ba, Bb = featmajor(s2, 0, mid=mid)
                post()
                for tb, bk in ((0, Ba), (1, Bb)):
                    P.op(DVE, lambda e, tb=tb, bk=bk: e.tensor_tensor(out=ysb[:, tb * 512:(tb + 1) * 512], in0=pbank[bk][:],
                                                                      in1=ysb[:, tb * 512:(tb + 1) * 512], op=ALU.mult),
                         r=[r_pb[bk], r_cv[2]], w=[r_cv[2]])
                side()
                za, zb = featmajor(s2, 128, mid=mid)
                post()
                for tb, bk in ((0, za), (1, zb)):
                    P.op(ACT, lambda e, tb=tb, bk=bk: e.activation(out=szb[:, tb * 512:(tb + 1) * 512], in_=pbank[bk][:],
                                                                   func=AF.Silu), r=[r_pb[bk]], w=[r_cv[3]])
                if unit["k"] < 19 + P2LAG:
                    P.op(ACT, lambda e: e.activation(out=stat[:, 61:62], in_=stat[:, 60:61], func=AF.Ln, bias=1.0),
                         w=[r_dummy])
                P.op(DVE, lambda e, g=g: e.tensor_tensor(out=yT[:, g, :], in0=ysb, in1=szb, op=ALU.mult),
                     r=[r_cv[2], r_cv[3]], w=[r_yT[g][i] for i in range(NT)])
                wdone()
                if limit >= 5 and g in WSCHED_END:
                    for cb in WSCHED_END[g]:
                        P.dma(POOL, woutsb[:, :, cb * 256:(cb + 1) * 256],
                              wout_d[:, cb * 256:(cb + 1) * 256].rearrange("(m p) c -> p m c", p=128), w=[r_wout[cb]])
            while unit["k"] < 18 + P2LAG:
                side()
                mid()
                post()
            if limit < 5:
                return

            fgv = wsl[0][:].rearrange("p a b -> p (a b)").bitcast(F32)
            P.dma(SP, fgv, fg_d, w=[r_fg])

            def x5_load(i):
                P.dma(SP, xt5[i % 2], x_d[2 + i * 128: 2 + (i + 1) * 128, :], w=[r_x5[i % 2]])
            x5_load(0)
            x5_load(1)
            for i in range(NT):
                b = i % 2
                base = 0 if i % 2 == 0 else 4
                for c4 in range(4):
                    bk = base + c4

                    def om(e, i=i, c4=c4, bk=bk, ms=()):
                        ins = None
                        for m in ms:
                            ins = e.matmul(pbank[bk][:], lhsT=yT[:, m, i * 128:(i + 1) * 128],
                                           rhs=woutsb[:, m, c4 * 512:(c4 + 1) * 512], start=(m == 8), stop=(m == 7))
                        return ins
                    ms1, ms2 = list(range(8, 16)), list(range(0, 8))
                    P.op(PE, lambda e, om=om, ms1=ms1: om(e, ms=ms1),
                         r=[r_yT[m][i] for m in ms1] + [r_wout[2 * c4], r_wout[2 * c4 + 1]], w=[r_pb[bk]])
                    P.op(PE, lambda e, om=om, ms2=ms2: om(e, ms=ms2),
                         r=[r_yT[m][i] for m in ms2] + [r_wout[2 * c4], r_wout[2 * c4 + 1]], w=[r_pb[bk]])
                    P.op(DVE, lambda e, b=b, c4=c4, bk=bk: e.tensor_tensor(out=xo5[b][:, c4 * 512:(c4 + 1) * 512],
                                                                           in0=pbank[bk][:],
                                                                           in1=xt5[b][:, c4 * 512:(c4 + 1) * 512], op=ALU.add),
                         r=[r_pb[bk], r_x5[b]], w=[r_xo[b]], join=(c4 > 0))
                    if i == NT - 1:
                        cs = slice(c4 * 512, (c4 + 1) * 512)
                        P.op(ACT, lambda e, b=b, c4=c4, cs=cs: e.activation(out=xt5[b][:, cs], in_=xo5[b][:, cs], func=AF.Square,
                                                                            accum_out=stat[:, 44 + c4:45 + c4]),
                             r=[r_xo[b]], w=[r_x5[b], r_st5[i]], join=(c4 > 0))
                        P.op(DVE, lambda e, b=b, cs=cs: e.tensor_tensor(out=xo5[b][:, cs], in0=xo5[b][:, cs], in1=fgv[:, cs],
                                                                        op=ALU.mult),
                             r=[r_xo[b], r_fg], w=[r_xo5p[c4]])
                sc = stat[:, 32 + i:33 + i]
                if i == NT - 1:
                    P.op(DVE, lambda e, sc=sc: e.tensor_reduce(out=sc, in_=stat[:, 44:48], axis=mybir.AxisListType.X, op=ALU.add),
                         r=[r_st5[i]], w=[r_st5[i]])
                    P.op(ACT, lambda e, sc=sc: e.activation(out=sc, in_=sc, func=AF.Ln, scale=1.0 / D, bias=EPS),
                         r=[r_st5[i]], w=[r_st5[i]])
                    P.op(ACT, lambda e, sc=sc: e.activation(out=sc, in_=sc, func=AF.Exp, scale=-0.5), r=[r_st5[i]], w=[r_st5[i]])
                    for c4 in (3, 0, 2, 1):
                        cs = slice(c4 * 512, (c4 + 1) * 512)
                        if c4 in (3, 2):
                            P.op(DVE, lambda e, b=b, sc=sc, cs=cs: e.tensor_scalar(out=xo5[b][:, cs], in0=xo5[b][:, cs], scalar1=sc,
                                                                                   scalar2=None, op0=ALU.mult),
                                 r=[r_xo5p[c4], r_st5[i]], w=[r_xo5p[c4]])
                        else:
                            P.op(ACT, lambda e, b=b, sc=sc, cs=cs: e.activation(out=xo5[b][:, cs], in_=xo5[b][:, cs], func=AF.Copy,
                                                                                scale=sc),
                                 r=[r_xo5p[c4], r_st5[i]], w=[r_xo5p[c4]])
                        stores.append(P.dma(SP, y_d[i * 128:(i + 1) * 128, cs], xo5[b][:, cs], r=[r_xo5p[c4]]))
                    continue
                P.op(ACT, lambda e, b=b, sc=sc: e.activation(out=xt5[b], in_=xo5[b], func=AF.Square, accum_out=sc),
                     r=[r_xo[b]], w=[r_x5[b], r_st5[i]])
                P.op(ACT, lambda e, sc=sc: e.activation(out=sc, in_=sc, func=AF.Ln, scale=1.0 / D, bias=EPS),
                     r=[r_st5[i]], w=[r_st5[i]])
                P.op(ACT, lambda e, sc=sc: e.activation(out=sc, in_=sc, func=AF.Exp, scale=-0.5), r=[r_st5[i]], w=[r_st5[i]])
                if i + 2 < NT:
                    x5_load(i + 2)
                P.op(DVE, lambda e, b=b, sc=sc: e.scalar_tensor_tensor(out=xo5[b], in0=xo5[b], scalar=sc, in1=fgv,
                                                                       op0=ALU.mult, op1=ALU.mult),
                     r=[r_xo[b], r_st5[i], r_fg], w=[r_xo[b]])
                stores.append(P.dma(SP, y_d[i * 128:(i + 1) * 128, :], xo5[b], r=[r_xo[b]]))

        body()
        if limit < 5:
            stores.append(P.dma(SP, y_d[0:128, :], regA[:, 0:2048], r=ph0A + ph1A + cvA + p2all))
        if debug:
            if limit == 0:
                dump("hT", hT[:], [128, NDC, TOK], BF16, r_hT)
                dump("hTh", hTh[:], [128, NDC, 2], BF16, [r_hTh])
            if limit == 1:
                dump("wdec", wdec, [128, NT, 512], F32, r_wdec)
                dump("dec", dec[:], [128, 64], F32, [r_dec])
                dump("dtot", dtot[:], [128, 8], F32, [r_dec])
                dump("kdec", kdec, [128, NT, 512], BF16, r_kdec)
                dump("v", vsb, [128, NT, 1024], BF16, r_v)
                dump("qT", qT, [128, 4, 1024], BF16, r_qT)
            if limit in (3, 4):
                dump("S", S[:], [128, 4, 256], F32, r_S)
                dump("rT", rT, [128, 8, 1024], BF16, r_rT)
                dump("yT", yT[:], [128, 16, TOK], BF16, [x for l in r_yT for x in l])
            for (t, ap, res) in dbgs:
                stores.append(P.dma(SP, t, ap, r=res))
        P.emit(final_waits=stores)
    return nc, P


_CACHE = {}


def _consts():
    c = np.zeros((128, 264), np.float32)
    p = np.arange(128)
    same = (p[:, None] // 64) == (p[None, :] // 64)
    c[:, 0:128] = ((p[:, None] > p[None, :]) & same).astype(np.float32)
    c[:, 128] = (p < 64)
    c[:, 129] = (p >= 64)
    c[:, 130:258] = 1.0
    idb = np.zeros((128, 256), np.float32)
    idb[:, 0:128] = np.eye(128)
    idb[:, 128:256] = 1.0
    return c, idb.astype(ml_dtypes.bfloat16)


def kernel(x, norm_g, w_in, conv_w, conv_b, gla_w_up, gla_b_gate, gla_norm_g, w_out, final_g):
    x = np.asarray(x, np.float32)
    if "nc" not in _CACHE:
        _CACHE["nc"] = build()[0]
    nc = _CACHE["nc"]
    w_in0 = np.ascontiguousarray(np.asarray(w_in, np.float32)[0])
    w_out0 = np.ascontiguousarray(np.asarray(w_out, np.float32)[0])
    gb = np.ascontiguousarray(np.asarray(norm_g, np.float32)[0][None, :])
    fgb = np.ascontiguousarray(np.broadcast_to(np.asarray(final_g, np.float32)[None, :], (128, D)))
    pfm = np.zeros((128, 48), np.float32)
    cwv = np.asarray(conv_w, np.float32)[0]
    for j in range(3):
        pfm[:, j * 8:(j + 1) * 8] = cwv[j].reshape(8, 128).T
    pfm[:, 24:32] = np.asarray(conv_b, np.float32)[0].reshape(8, 128).T
    pfm[:, 32:40] = np.asarray(gla_norm_g, np.float32)[0].reshape(8, 128).T
    wup = np.ascontiguousarray(np.asarray(gla_w_up, np.float32)[0])
    bgt = np.ascontiguousarray(np.asarray(gla_b_gate, np.float32)[0][None, :])
    cst0, idb = _consts()
    in_maps = []
    for c in range(8):
        b, q = c // 4, c % 4
        xc = np.zeros((TOK + 2, D), np.float32)
        xc[2:] = x[b, q * TOK:(q + 1) * TOK]
        if q > 0:
            xc[0:2] = x[b, q * TOK - 2:q * TOK]
        cst = cst0.copy()
        for r in range(3):
            cst[:, 258 + r] = 1.0 if r < q else 0.0
            cst[:, 261 + r] = 0.0 if r < q else 1.0
        in_maps.append({"x": xc, "w_in": w_in0, "w_out": w_out0, "norm_g1": gb, "final_gb": fgb, "pfm": pfm,
                        "w_up": wup, "b_gate": bgt, "cst": cst, "identb": idb})
    res = run_bass_kernel_spmd(nc, in_maps, core_ids=list(range(8)))
    out = np.zeros((2, 4096, D), np.float32)
    for c in range(8):
        b, q = c // 4, c % 4
        out[b, q * TOK:(q + 1) * TOK] = res.results[c]["y"]
    return out
```
